# Optimizing a Trainium2 kernel written in Bass

```python
import jax, jax.numpy as jnp
from jax import lax
import numpy as np


D_MODEL = 1024
BATCH = 8
SEQ = 4096
DEPTH = 4
DEC_BATCH = 4
DEC_SEQ = 4096
PAST_LEN = 128

GRID_W = 64
N_EVEN = (DEPTH + 1) // 2
N_ODD = DEPTH // 2
EPS = 1e-6

POOL_WIDTH = D_MODEL // 2
POOL_WINDOWS = (2, 4, 8, 16)
POOL_GROUPS = len(POOL_WINDOWS)
POOL_GC = POOL_WIDTH // POOL_GROUPS

HEAD_DIM = 64
N_Q_HEADS = (D_MODEL // 2) // HEAD_DIM
N_KV_HEADS = 2
Q_PER_KV = N_Q_HEADS // N_KV_HEADS
ATTN_WIDTH = N_Q_HEADS * HEAD_DIM
KV_WIDTH = N_KV_HEADS * HEAD_DIM
ROPE_HALF = HEAD_DIM // 2
ROPE_FREQ = ROPE_HALF // 2
ROPE_THETA = 10000.0
Q_BLOCK = 128

EVEN_SIZES = (POOL_WIDTH, POOL_WIDTH, ATTN_WIDTH, KV_WIDTH, KV_WIDTH, ATTN_WIDTH)
EVEN_IN = sum(EVEN_SIZES)
EVEN_SPLITS = tuple(int(s) for s in np.cumsum(EVEN_SIZES)[:-1])
EVEN_MIX = POOL_WIDTH + ATTN_WIDTH

SGU_WIDTH = D_MODEL
SGU_GROUPS = 8
SGU_GC = SGU_WIDTH // SGU_GROUPS
CHUNK = 128
ODD_IN = 3 * SGU_WIDTH

kernel_name = 'hybrid_pool_axialgqa_gmlp_encoder'


def rms_norm(x, g):
    xf = x.astype(jnp.float32)
    y = xf * lax.rsqrt(jnp.mean(xf * xf, axis=-1, keepdims=True) + EPS)
    return (y * g.astype(jnp.float32)).astype(x.dtype)


def rope_tables(L):
    rows_n = L // GRID_W
    row = jnp.repeat(jnp.arange(rows_n), GRID_W).astype(jnp.float32)
    col = jnp.tile(jnp.arange(GRID_W), rows_n).astype(jnp.float32)
    inv = 1.0 / (ROPE_THETA ** (jnp.arange(ROPE_FREQ, dtype=jnp.float32) / ROPE_FREQ))
    ang_r = row[:, None] * inv[None, :]
    ang_c = col[:, None] * inv[None, :]
    return (jnp.cos(ang_r), jnp.sin(ang_r), jnp.cos(ang_c), jnp.sin(ang_c))


def rope_1d(x, cos, sin):
    x1, x2 = x[..., :ROPE_FREQ], x[..., ROPE_FREQ:]
    return jnp.concatenate([x1 * cos - x2 * sin, x2 * cos + x1 * sin], axis=-1)


def rope_2d(x, tables):
    cos_r, sin_r, cos_c, sin_c = tables
    xf = x.astype(jnp.float32)
    xr = rope_1d(xf[..., :ROPE_HALF], cos_r[:, None, :], sin_r[:, None, :])
    xc = rope_1d(xf[..., ROPE_HALF:], cos_c[:, None, :], sin_c[:, None, :])
    return jnp.concatenate([xr, xc], axis=-1).astype(x.dtype)


def pool_mixer(u, pool_w, pool_scale):
    B, L, _ = u.shape
    uf = u.astype(jnp.float32)
    cs = jnp.concatenate([jnp.zeros((B, 1, POOL_WIDTH), jnp.float32), jnp.cumsum(uf, axis=1)], axis=1)
    t = jnp.arange(L)
    means = []
    for g, w in enumerate(POOL_WINDOWS):
        lo = jnp.clip(t - w // 2, 0, L)
        hi = jnp.clip(t + w // 2, 0, L)
        cnt = (hi - lo).astype(jnp.float32)
        csg = cs[..., g * POOL_GC:(g + 1) * POOL_GC]
        means.append((jnp.take(csg, hi, axis=1) - jnp.take(csg, lo, axis=1)) / cnt[None, :, None])
    pooled = jnp.stack(means, axis=2)
    diff = (pooled - uf.reshape(B, L, POOL_GROUPS, POOL_GC)).astype(u.dtype)
    mixed = jnp.einsum('blgc,gcd->blgd', diff, pool_w).reshape(B, L, POOL_WIDTH)
    return mixed * pool_scale


def axial_gqa(q, k, v, q_g, k_g, tables):
    B, L, _ = q.shape
    nb = L // Q_BLOCK
    q = rope_2d(rms_norm(q.reshape(B, L, N_Q_HEADS, HEAD_DIM), q_g), tables)
    k = rope_2d(rms_norm(k.reshape(B, L, N_KV_HEADS, HEAD_DIM), k_g), tables)
    v = v.reshape(B, L, N_KV_HEADS, HEAD_DIM)
    qb = q.reshape(B, nb, Q_BLOCK, N_KV_HEADS, Q_PER_KV, HEAD_DIM).transpose(1, 0, 2, 3, 4, 5)
    scale = HEAD_DIM ** -0.5

    def block(q_blk):
        s = jnp.einsum('bqkgd,bskd->bkgqs', q_blk, k, preferred_element_type=jnp.float32) * scale
        p = jax.nn.softmax(s, axis=-1).astype(v.dtype)
        return jnp.einsum('bkgqs,bskd->bqkgd', p, v)

    o = lax.map(block, qb)
    return o.transpose(1, 0, 2, 3, 4, 5).reshape(B, L, ATTN_WIDTH)


def even_layer(x, norm_g, w_in, pool_w, pool_scale, q_g, k_g, w_out, tables):
    h = rms_norm(x, norm_g)
    p = h @ w_in
    pu, pz, q, k, v, az = jnp.split(p, EVEN_SPLITS, axis=-1)
    a_out = pool_mixer(pu, pool_w, pool_scale) * jax.nn.silu(pz)
    b_out = axial_gqa(q, k, v, q_g, k_g, tables) * jax.nn.silu(az)
    return x + jnp.concatenate([a_out, b_out], axis=-1) @ w_out


def odd_layer(x, norm_g, w_in, sgu_g, w_s, b_s, w_out):
    B, L, _ = x.shape
    h = rms_norm(x, norm_g)
    u, vv, z = jnp.split(h @ w_in, 3, axis=-1)
    u = jax.nn.gelu(u, approximate=False)
    vv = rms_norm(jax.nn.gelu(vv, approximate=False), sgu_g)
    vc = vv.reshape(B, L // CHUNK, CHUNK, SGU_GROUPS, SGU_GC)
    sv = jnp.einsum('hpq,bnqhc->bnphc', w_s, vc) + b_s.T[None, None, :, :, None]
    y = u * sv.reshape(B, L, SGU_WIDTH) * jax.nn.silu(z)
    return x + y @ w_out


def trunk(x, norm_e, w_in_e, pool_w, pool_scale, q_norm, k_norm, w_out_e,
          norm_o, w_in_o, sgu_norm, w_s, b_s, w_out_o):
    tables = rope_tables(x.shape[1])
    for i in range(DEPTH):
        j = i // 2
        if i % 2 == 0:
            x = even_layer(x, norm_e[j], w_in_e[j], pool_w[j], pool_scale[j], q_norm[j], k_norm[j], w_out_e[j], tables)
        else:
            x = odd_layer(x, norm_o[j], w_in_o[j], sgu_norm[j], w_s[j], b_s[j], w_out_o[j])
    return x


def setup_inputs(seed: int = 0) -> dict:
    key = jax.random.key(seed)
    ks = jax.random.split(key, 16)
    f32 = jnp.float32
    nrm = lambda k, shape, s: jax.random.normal(k, shape, f32) * s
    return {
        'x_prompt': jax.random.normal(ks[0], (BATCH, SEQ, D_MODEL), f32),
        'x_sample': jax.random.normal(ks[1], (DEC_BATCH, DEC_SEQ, D_MODEL), f32),
        'norm_e': 1.0 + nrm(ks[2], (N_EVEN, D_MODEL), 0.02),
        'w_in_e': nrm(ks[3], (N_EVEN, D_MODEL, EVEN_IN), D_MODEL ** -0.5),
        'pool_w': nrm(ks[4], (N_EVEN, POOL_GROUPS, POOL_GC, POOL_GC), POOL_GC ** -0.5),
        'pool_scale': 1.0 + nrm(ks[5], (N_EVEN, POOL_WIDTH), 0.02),
        'q_norm': 1.0 + nrm(ks[6], (N_EVEN, HEAD_DIM), 0.02),
        'k_norm': 1.0 + nrm(ks[7], (N_EVEN, HEAD_DIM), 0.02),
        'w_out_e': nrm(ks[8], (N_EVEN, EVEN_MIX, D_MODEL), EVEN_MIX ** -0.5),
        'norm_o': 1.0 + nrm(ks[9], (N_ODD, D_MODEL), 0.02),
        'w_in_o': nrm(ks[10], (N_ODD, D_MODEL, ODD_IN), D_MODEL ** -0.5),
        'sgu_norm': 1.0 + nrm(ks[11], (N_ODD, SGU_WIDTH), 0.02),
        'w_s': nrm(ks[12], (N_ODD, SGU_GROUPS, CHUNK, CHUNK), CHUNK ** -0.5),
        'b_s': 1.0 + nrm(ks[13], (N_ODD, SGU_GROUPS, CHUNK), 0.02),
        'w_out_o': nrm(ks[14], (N_ODD, SGU_WIDTH, D_MODEL), SGU_WIDTH ** -0.5),
    }


def reference(x_prompt, x_sample, norm_e, w_in_e, pool_w, pool_scale, q_norm, k_norm, w_out_e,
              norm_o, w_in_o, sgu_norm, w_s, b_s, w_out_o):
    y_prompt = trunk(x_prompt, norm_e, w_in_e, pool_w, pool_scale, q_norm, k_norm, w_out_e,
                     norm_o, w_in_o, sgu_norm, w_s, b_s, w_out_o)
    y_sample = trunk(x_sample, norm_e, w_in_e, pool_w, pool_scale, q_norm, k_norm, w_out_e,
                     norm_o, w_in_o, sgu_norm, w_s, b_s, w_out_o)
    return (y_prompt, y_sample)
```

```python
import contextlib
import numpy as np
import ml_dtypes
import concourse.bass as bass
import concourse.mybir as mybir
from concourse.bass_utils import run_bass_kernel_spmd

F32 = mybir.dt.float32
BF16 = mybir.dt.bfloat16
AF = mybir.ActivationFunctionType
ALU = mybir.AluOpType
AX = mybir.AxisListType

ENGS = ("pe", "act", "dve", "pool", "sp")

D = 1024
EPS = 1e-6
GRID_W = 64
POOL_WINDOWS = (2, 4, 8, 16)
VW = 66
N_CORES = 8


class Prog:
    def __init__(self, nc):
        self.nc = nc
        self.streams = {e: [] for e in ENGS}
        self.count = {e: 0 for e in ENGS}
        self.seen = {e: {} for e in ENGS}
        self.last_w = {}
        self.readers = {}
        self.dma_count = {}
        self.sem_keys = []
        self.alias = {}

    def _semkey(self, k):
        if k not in self.sem_keys:
            self.sem_keys.append(k)
        return k

    def _deps(self, eng, reads, writes):
        evs = {}
        reads = [self.alias.get(r, r) for r in reads]
        writes = [self.alias.get(w, w) for w in writes]

        def add(ev):
            if ev is None:
                return
            k, v, e = ev
            if e == "pe" and eng == "pe":
                return
            if v > evs.get(k, 0):
                evs[k] = v

        for r in reads:
            add(self.last_w.get(r))
        for w in writes:
            add(self.last_w.get(w))
            for k, (v, e) in self.readers.get(w, {}).items():
                add((k, v, e))
        out = []
        seen = self.seen[eng]
        for k, v in evs.items():
            if seen.get(k, 0) >= v:
                continue
            seen[k] = v
            out.append((k, v))
        return out

    def _record(self, ev, reads, writes):
        k, v, e = ev
        reads = [self.alias.get(r, r) for r in reads]
        writes = [self.alias.get(w, w) for w in writes]
        for r in reads:
            self.readers.setdefault(r, {})[k] = (v, e)
        for w in writes:
            self.last_w[w] = ev
            self.readers[w] = {}

    def op(self, eng, fn, reads=(), writes=()):
        waits = self._deps(eng, reads, writes)
        self.count[eng] += 1
        k = self._semkey("e:" + eng)
        ev = (k, self.count[eng], eng)
        self.streams[eng].append((waits, fn, (k, 1)))
        self._record(ev, reads, writes)
        return ev

    def dma(self, eng, fn, key, reads=(), writes=(), n=1):
        waits = self._deps(eng, reads, writes)
        k = self._semkey("d:" + key)
        self.dma_count[k] = self.dma_count.get(k, 0) + 16 * n
        ev = (k, self.dma_count[k], "dma")
        self.streams[eng].append((waits, fn, (k, 16)))
        self._record(ev, reads, writes)
        return ev

    def cc(self, fn, key, reads=(), writes=()):
        waits = self._deps("pool", reads, writes)
        k = self._semkey("c:" + key)
        self.dma_count[k] = self.dma_count.get(k, 0) + 1
        ev = (k, self.dma_count[k], "dma")
        self.streams["pool"].append((waits, fn, (k, 1)))
        self._record(ev, reads, writes)
        return ev

    def emit(self, final_wait_eng="sp"):
        nc = self.nc
        with contextlib.ExitStack() as st:
            sems = {}
            for k in self.sem_keys:
                sems[k] = st.enter_context(nc.semaphore(k.replace(":", "_")))
            block = st.enter_context(nc.Block())
            finals = list(self.dma_count.items())

            def run(engine, name):
                for waits, fn, (k, inc) in self.streams[name]:
                    for wk, wv in waits:
                        engine.wait_ge(sems[wk], wv)
                    res = fn(engine)
                    if not isinstance(res, (list, tuple)):
                        res = [res]
                    for ins in res:
                        ins.then_inc(sems[k], inc)
                if name == final_wait_eng:
                    for fk, fv in finals:
                        engine.wait_ge(sems[fk], fv)
                    for e2 in ("pe", "act", "dve", "pool"):
                        kk = "e:" + e2
                        if kk in sems and self.count[e2] > 0:
                            engine.wait_ge(sems[kk], self.count[e2])

            @block.tensor
            def _(e):
                run(e, "pe")

            @block.scalar
            def _(e):
                run(e, "act")

            @block.vector
            def _(e):
                run(e, "dve")

            @block.gpsimd
            def _(e):
                run(e, "pool")

            @block.sync
            def _(e):
                run(e, "sp")


def rope_table(L):
    t = np.arange(L)
    row = (t // GRID_W).astype(np.float32)
    col = (t % GRID_W).astype(np.float32)
    inv = (1.0 / (np.float32(10000.0) ** (np.arange(16, dtype=np.float32) / np.float32(16)))).astype(np.float32)
    ar = (row[:, None] * inv[None, :]).astype(np.float32)
    ac = (col[:, None] * inv[None, :]).astype(np.float32)
    cr, sr, cc, sc = np.cos(ar), np.sin(ar), np.cos(ac), np.sin(ac)
    C = np.concatenate([cr, cr, cc, cc], axis=1)
    S = np.concatenate([-sr, sr, -sc, sc], axis=1)
    return np.ascontiguousarray(np.concatenate([C, S], axis=1).astype(np.float32))


def band_tables():
    bc = np.zeros((128, 12, 128), np.float32)
    bp = np.zeros((128, 4, 8), np.float32)
    bn = np.zeros((128, 4, 8), np.float32)
    for g, w in enumerate(POOL_WINDOWS):
        hw = w // 2
        for var in range(3):
            for t in range(128):
                if var == 0:
                    lo, hi = max(t - hw, 0), t + hw
                elif var == 1:
                    lo, hi = t - hw, t + hw
                else:
                    lo, hi = t - hw, min(t + hw, 128)
                cnt = hi - lo
                for s_ in range(max(lo, 0), min(hi, 128)):
                    bc[s_, g * 3 + var, t] += 1.0 / cnt
                bc[t, g * 3 + var, t] -= 1.0
        for t in range(128):
            lo, hi = t - hw, t + hw
            for s_ in range(lo, 0):
                bp[128 + s_, g, t] += 1.0 / w
            for s_ in range(128, hi):
                bn[s_ - 128, g, t - 120] += 1.0 / w
    bf = ml_dtypes.bfloat16
    return bc.astype(bf), bp.astype(bf), bn.astype(bf)


def half_tiles(NT):
    return NT // 2 + 2


def build_program(L, NSEQ, depth=4, half=True):
    NT = L // 128
    assert NT >= 4 and NSEQ == 2
    NTL = half_tiles(NT) if half else NT
    HT = NT // 2
    nc = bass.Bass("TRN2", target_bir_lowering=False)

    def din(name, shape, dt=F32):
        return nc.dram_tensor(name, list(shape), dt, kind="ExternalInput").ap()

    x_d = din("x", [NSEQ, L, D])
    norm_e = din("norm_e", [2, D])
    w_in_e = din("w_in_e", [2, D, 2304])
    pool_w = din("pool_w", [2, 4, 128, 128])
    pool_scale = din("pool_scale", [2, 512])
    q_norm = din("q_norm", [2, 64])
    k_norm = din("k_norm", [2, 64])
    w_out_e = din("w_out_e", [2, D, D])
    norm_o = din("norm_o", [2, D])
    w_in_o = din("w_in_o", [2, D, 3072])
    sgu_norm = din("sgu_norm", [2, D])
    w_s = din("w_s", [2, 8, 128, 128])
    b_s = din("b_s", [2, 8, 128])
    w_out_o = din("w_out_o", [2, D, D])
    ident_b = din("ident_b", [128, 128], BF16)
    ident_f = din("ident_f", [128, 128], F32)
    rope_d = din("rope", [L, 128], F32)
    bc_d = din("band_c", [128, 12, 128], BF16)
    bp_d = din("band_p", [128, 4, 8], BF16)
    bn_d = din("band_n", [128, 4, 8], BF16)
    xh_d = din("xh", [NTL * 128, D])
    ropeh_d = din("rope_h", [NTL * 128, 128], F32)
    bch_d = din("band_ch", [128, 12, 128], BF16)

    out_d = nc.dram_tensor("out", [L, D], F32, kind="ExternalOutput").ap()
    outh_d = nc.dram_tensor("out_h", [NTL * 128, D], F32, kind="ExternalOutput").ap()
    xs_d = nc.dram_tensor("xs", [L, D], F32, kind="Internal").ap()
    xsh_d = nc.dram_tensor("xsh", [NTL * 128, D], F32, kind="Internal").ap()
    kvl_k = nc.dram_tensor("kvl_k", [128, NTL * 128], BF16, kind="Internal")
    kvl_v = nc.dram_tensor("kvl_v", [NTL * 128, 128], BF16, kind="Internal")
    kvg_k = nc.dram_tensor("kvg_k", [2 * 128, NTL * 128], BF16, kind="Internal")
    kvg_v = nc.dram_tensor("kvg_v", [2 * NTL * 128, 128], BF16, kind="Internal")
    kts_d = nc.dram_tensor("kts", [NSEQ, 128, L], BF16, kind="Internal").ap()
    vs_d = nc.dram_tensor("vs", [NSEQ, L, 128], BF16, kind="Internal").ap()

    with contextlib.ExitStack() as st:
        def sb(name, shape, dt):
            return st.enter_context(nc.sbuf_tensor("s_" + name, list(shape), dt))

        def ps(name, shape, dt):
            return st.enter_context(nc.psum_tensor("p_" + name, list(shape), dt))

        WinE = sb("WinE", [128, 8, 2304], BF16)
        WoutE = sb("WoutE", [128, 8, 1024], BF16)
        poolW = sb("poolW", [128, 4, 128], BF16)
        WinO = sb("WinO", [128, 8, 3072], BF16)
        WoutO = sb("WoutO", [128, 8, 1024], BF16)
        WsT = sb("WsT", [128, 8, 128], BF16)
        gS = sb("gS", [128, D], F32)
        gcE = sb("gcE", [128, 8], F32)
        gcO = sb("gcO", [128, 8], F32)
        gcEn = sb("gcEn", [128, 8], F32)
        gq = sb("gq", [128, 64], F32)
        gk = sb("gk", [128, 64], F32)
        bsT = sb("bsT", [128, 8], F32)
        idb = sb("idb", [128, 128], BF16)
        idf = sb("idf", [128, VW], F32)
        bandc = sb("bandc", [128, 12, 128], BF16)
        bandch = sb("bandch", [128, 12, 128], BF16)
        bandp = sb("bandp", [128, 4, 8], BF16)
        bandn = sb("bandn", [128, 4, 8], BF16)
        mhalf = sb("mhalf", [128, 8], F32)
        KT = sb("KT", [128, L], BF16)
        V = sb("V", [128, NT, 2, VW], BF16)
        xs1 = sb("xs1", [128, D], F32)
        rp1 = sb("rp1", [128, 128], F32)
        hT1 = sb("hT1", [128, D], BF16)
        scr1 = sb("scr1", [128, D], F32)
        qf = scr1[:, 0:512]
        qa = scr1[:, 512:1024]
        qr = sb("qr", [128, 512], BF16)
        st8 = sb("st8", [128, 8], F32)
        rs8 = sb("rs8", [128, 8], F32)
        pu = [sb(f"pu{i}", [128, 512], BF16) for i in range(4)]
        spz = [sb(f"spz{i}", [128, 512], BF16) for i in range(3)]
        saz = [sb(f"saz{i}", [128, 512], BF16) for i in range(3)]
        QTz = [[sb(f"QTz{i}{g}", [128, 512], BF16) for g in range(2)] for i in range(2)]
        th = sb("th", [128, 512], F32)
        tmpf = sb("tmpf", [128, 512], F32)
        st1a = sb("st1a", [128, 1], F32)
        rs1a = sb("rs1a", [128, 1], F32)
        st1 = sb("st1", [128, 1], F32)
        rs1 = sb("rs1", [128, 1], F32)
        pT = [sb(f"pT{i}", [128, 512], BF16) for i in range(3)]
        xp = sb("xp", [128, D], F32)
        hT2 = sb("hT2", [128, D], BF16)
        scr = sb("scr", [128, D], F32)
        scr2 = sb("scr2", [128, D], F32)
        vv = scr[:, :]
        u = scr2[:, :]
        oT = [scr2[:, i * 512:(i + 1) * 512] for i in range(2)]
        den = sb("den", [128, 8], F32)
        rden = sb("rden", [128, 8], F32)
        diffT = sb("diffT", [128, 512], BF16)
        mix = sb("mix", [128, D], BF16)
        mixT = sb("mixT", [128, D], BF16)
        vvn = mixT
        kf = sb("kf", [128, 128], F32)
        ka = sb("ka", [128, 128], F32)
        kr = sb("kr", [128, 128], BF16)
        st8p = sb("st8p", [128, 8], F32)
        rs8p = sb("rs8p", [128, 8], F32)
        kst = sb("kst", [128, 128], BF16)
        vst = sb("vst", [128, 128], BF16)
        kst2 = sb("kst2", [128, 128], BF16)
        vst2 = sb("vst2", [128, 128], BF16)
        xq = sb("xq", [128, D], F32)
        rpq = sb("rpq", [128, 128], F32)
        hT3 = sb("hT3", [128, D], BF16)
        st1c = sb("st1c", [128, 1], F32)
        rs1c = sb("rs1c", [128, 1], F32)
        psT = ps("psT", [128, 1024], BF16)
        psMM = [ps(f"psMM{i}", [128, 512], F32) for i in range(2)]
        psS = [ps(f"psS{i}", [128, 512], F32) for i in range(3)]
        psO = [ps(f"psO{i}", [128, 512], F32) for i in range(2)]

        P = Prog(nc)
        P.alias = {"qf": "scr1", "qa": "scr1", "vv": "scr", "oT0": "scr2", "oT1": "scr2", "u": "scr2",
                   "vvn": "mixT", "wsn": "mix", "Wkv": "WinE"}
        wsn = mix[:, :].rearrange("p (h q) -> p h q", h=8)
        Wkv = WinE[:, :, 1536:1792]
        mmi = [0]

        def next_mm():
            i = mmi[0] % 2
            mmi[0] += 1
            return psMM[i], f"psMM{i}"

        def castload(dst, dst_name, src, ncols, chunk=1024):
            srcv = src.rearrange("(k p) n -> p k n", p=128)
            chunks = [(c, min(c + chunk, ncols)) for c in range(0, ncols, chunk)]

            def fn(e):
                return [e.dma_start(out=dst[:, :, a:b], in_=srcv[:, :, a:b]) for a, b in chunks]
            P.dma("pool", fn, dst_name, writes=[dst_name], n=len(chunks))

        def bcastload(dst, dst_name, src_row, eng="sp"):
            P.dma(eng, lambda e: e.dma_start(out=dst[:], in_=src_row.partition_broadcast(128)), dst_name,
                  writes=[dst_name])

        def colload(dst, dst_name, src_vec):
            P.dma("sp", lambda e: e.dma_start(out=dst[:], in_=src_vec.rearrange("(k p) -> p k", p=128),
                                              allow_slow_non_contiguous=True), dst_name, writes=[dst_name])

        def rowscale(W, wname, gc, gname, ncols):
            for k in range(8):
                P.op("dve", lambda e, k=k: e.tensor_scalar(out=W[:, k, 0:ncols], in0=W[:, k, 0:ncols],
                                                           scalar1=gc[:, k:k + 1], scalar2=None, op0=ALU.mult),
                     reads=[wname, gname], writes=[wname])

        P.dma("sp", lambda e: e.dma_start(out=idb[:], in_=ident_b), "idb", writes=["idb"])
        P.dma("sp", lambda e: e.dma_start(out=idf[:], in_=ident_f[:, 0:VW]), "idf", writes=["idf"])
        P.dma("sp", lambda e: e.dma_start(out=bandc[:], in_=bc_d), "bandc", writes=["bandc"])
        P.dma("sp", lambda e: e.dma_start(out=bandch[:], in_=bch_d), "bandch", writes=["bandch"])
        P.dma("sp", lambda e: e.dma_start(out=bandp[:], in_=bp_d), "bandp", writes=["bandp"])
        P.dma("sp", lambda e: e.dma_start(out=bandn[:], in_=bn_d), "bandn", writes=["bandn"])
        P.op("pool", lambda e: e.memset(mhalf[:], -0.5), writes=["mhalf"])
        for i in range(2):
            for g in range(2):
                P.op("pool", lambda e, i=i, g=g: e.memset(QTz[i][g][:], 0.0), writes=[f"QTz{i}"])

        def load_kv_weights(j):
            def fn(e):
                return [e.dma_start(out=Wkv, in_=w_in_e[j][:, 1536:1792].rearrange("(k p) n -> p k n", p=128))]
            P.dma("pool", fn, "WinE", writes=["WinE"])
            colload(gcEn, "gcEn", norm_e[j])
            for k in range(8):
                P.op("dve", lambda e, k=k: e.tensor_scalar(out=WinE[:, k, 1536:1792], in0=WinE[:, k, 1536:1792],
                                                           scalar1=gcEn[:, k:k + 1], scalar2=None, op0=ALU.mult),
                     reads=["WinE", "gcEn"], writes=["WinE"])
            bcastload(gk, "gk", k_norm[j:j + 1, :])

        def issue_pair_dmas(j):
            castload(WinE, "WinE", w_in_e[j], 2304)
            colload(gcE, "gcE", norm_e[j])
            bcastload(gq, "gq", q_norm[j:j + 1, :])
            bcastload(tmpf, "tmpf", pool_scale[j:j + 1, :])
            P.dma("pool", lambda e: e.dma_start(out=poolW[:], in_=pool_w[j].rearrange("g c d -> c g d")), "poolW",
                  writes=["poolW"])
            castload(WoutE, "WoutE", w_out_e[j], 1024)
            castload(WinO, "WinO", w_in_o[j], 3072)
            colload(gcO, "gcO", norm_o[j])
            bcastload(gS, "gS", sgu_norm[j:j + 1, :])
            P.dma("sp", lambda e: e.dma_start(out=bsT[:], in_=b_s[j].rearrange("h p -> p h"),
                                              allow_slow_non_contiguous=True), "bsT", writes=["bsT"])
            P.dma("pool", lambda e: e.dma_start(out=wsn, in_=w_s[j].rearrange("h p q -> p h q")), "wsn",
                  writes=["wsn"])
            castload(WoutO, "WoutO", w_out_o[j], 1024)

        def post_winE():
            rowscale(WinE, "WinE", gcE, "gcE", 2304)

        def post_rest():
            P.op("dve", lambda e: e.tensor_scalar(out=tmpf[:], in0=tmpf[:], scalar1=0.5, scalar2=None, op0=ALU.mult),
                 reads=["tmpf"], writes=["tmpf"])
            P.op("dve", lambda e: e.tensor_tensor(out=poolW[:].rearrange("c g d -> c (g d)"),
                                                  in0=poolW[:].rearrange("c g d -> c (g d)"), in1=tmpf[:], op=ALU.mult),
                 reads=["poolW", "tmpf"], writes=["poolW"])
            rowscale(WinO, "WinO", gcO, "gcO", 3072)
            for hh in range(8):
                P.op("pe", lambda e, hh=hh: e.transpose(out=psT[:, hh * 128:(hh + 1) * 128], in_=wsn[:, hh, :],
                                                        identity=idb[:]),
                     reads=["wsn", "idb"], writes=["psT"])
            P.op("dve", lambda e: e.tensor_copy(out=WsT[:].rearrange("p h q -> p (h q)"), in_=psT[:]),
                 reads=["psT"], writes=["WsT"])

        def gen_rmsnorm(xsrc, xname, hTdst, hTname, hb, hbn, s1, s1n, r1, r1n, ny=2):
            P.op("dve", lambda e: e.scalar_tensor_tensor(out=hb[:], in0=xsrc[:], scalar=1.0, in1=xsrc[:],
                                                         op0=ALU.mult, op1=ALU.mult, accum_out=s1[:]),
                 reads=[xname], writes=[hbn, s1n])
            for _ in range(ny):
                yield
            P.op("pool", lambda e: e.tensor_scalar(out=r1[:], in0=s1[:], scalar1=1.0 / D, scalar2=EPS,
                                                   op0=ALU.mult, op1=ALU.add),
                 reads=[s1n], writes=[r1n])
            P.op("pool", lambda e: e.tensor_tensor(out=r1[:], in0=r1[:], in1=mhalf[:, 0:1], op=ALU.pow),
                 reads=[r1n, "mhalf"], writes=[r1n])
            for _ in range(ny):
                yield
            P.op("dve", lambda e: e.tensor_scalar(out=hb[:], in0=xsrc[:], scalar1=r1[:], scalar2=None, op0=ALU.mult),
                 reads=[xname, r1n], writes=[hbn])
            for _ in range(ny):
                yield
            for k in range(8):
                P.op("pe", lambda e, k=k: e.transpose(out=psT[:, k * 128:(k + 1) * 128], in_=hb[:, k * 128:(k + 1) * 128],
                                                      identity=idb[:]),
                     reads=[hbn, "idb"], writes=["psT"])
            P.op("dve", lambda e: e.tensor_copy(out=hTdst[:], in_=psT[:]), reads=["psT"], writes=[hTname])
            yield

        def proj(W, wname, c0, ncols, lhs, lname):
            pm, pname = next_mm()
            for k in range(8):
                P.op("pe", lambda e, k=k, pm=pm: e.matmul(pm[:, 0:ncols], lhsT=lhs[:, k * 128:(k + 1) * 128],
                                                          rhs=W[:, k, c0:c0 + ncols], start=(k == 0), stop=(k == 7)),
                     reads=[lname, wname], writes=[pname])
            return pm, pname

        def head_norm_rope(bufs, names, nh, gain, gname, rope_t, rname, out_view, dname, s8, r8, s8n, r8n):
            src, ta = bufs
            sname, aname = names
            n = nh * 64
            s3 = src[:, 0:n].rearrange("p (h d) -> p h d", h=nh)
            a3 = ta[:, 0:n].rearrange("p (h d) -> p h d", h=nh)
            ceng0 = "dve" if nh == 2 else "pool"
            P.op(ceng0, lambda e: e.tensor_tensor(out=ta[:, 0:n], in0=src[:, 0:n], in1=src[:, 0:n], op=ALU.mult),
                 reads=[sname], writes=[aname])
            yield
            P.op("dve", lambda e: e.tensor_reduce(out=s8[:, 0:nh], in_=a3, axis=AX.X, op=ALU.add),
                 reads=[aname], writes=[s8n])
            P.op(ceng0, lambda e: e.tensor_scalar(out=r8[:, 0:nh], in0=s8[:, 0:nh], scalar1=1.0 / 64, scalar2=EPS,
                                                  op0=ALU.mult, op1=ALU.add),
                 reads=[s8n], writes=[r8n])
            P.op("pool", lambda e: e.tensor_tensor(out=r8[:, 0:nh], in0=r8[:, 0:nh], in1=mhalf[:, 0:nh], op=ALU.pow),
                 reads=[r8n, "mhalf"], writes=[r8n])
            yield
            P.op("dve", lambda e: e.tensor_tensor(out=s3, in0=s3, in1=r8[:, 0:nh].unsqueeze(2).to_broadcast([128, nh, 64]),
                                                  op=ALU.mult),
                 reads=[sname, r8n], writes=[sname])
            yield
            ceng = "dve" if nh == 2 else "pool"
            P.op(ceng, lambda e: e.tensor_tensor(out=s3, in0=s3, in1=gain[:].unsqueeze(1).to_broadcast([128, nh, 64]),
                                                 op=ALU.mult),
                 reads=[sname, gname], writes=[sname])
            yield
            a5 = ta[:, 0:n].rearrange("p (h a b c) -> p h a b c", h=nh, a=2, b=2)
            s5 = src[:, 0:n].rearrange("p (h a b c) -> p h a b c", h=nh, a=2, b=2)
            S5 = rope_t[:, 64:128].rearrange("p (a b c) -> p a b c", a=2, b=2)
            for blk in range(2):
                P.op(ceng, lambda e, blk=blk: e.tensor_tensor(
                    out=a5[:, :, :, blk, :], in0=s5[:, :, :, 1 - blk, :],
                    in1=S5[:, :, blk, :].unsqueeze(1).to_broadcast([128, nh, 2, 16]), op=ALU.mult),
                    reads=[sname, rname], writes=[aname])
            P.op("dve", lambda e: e.tensor_tensor(out=s3, in0=s3,
                                                  in1=rope_t[:, 0:64].unsqueeze(1).to_broadcast([128, nh, 64]), op=ALU.mult),
                 reads=[sname, rname], writes=[sname])
            yield
            if nh == 8:
                fa = ta[:, 0:n].rearrange("p (g j d) -> p g j d", g=2, j=4)
                fs = src[:, 0:n].rearrange("p (g j d) -> p g j d", g=2, j=4)
            else:
                fa, fs = a3, s3
            P.op("dve", lambda e: e.tensor_tensor(out=out_view, in0=fa, in1=fs, op=ALU.add),
                 reads=[aname, sname], writes=[dname])

        def silu2(pm, pname, dst, dname, scale_t=None, scale_name=None):
            P.op("act", lambda e: e.activation(out=th[:], in_=pm[:], func=AF.Tanh, scale=0.5),
                 reads=[pname], writes=["th"])
            if scale_t is None:
                P.op("dve", lambda e: e.scalar_tensor_tensor(out=dst, in0=th[:], scalar=1.0, in1=pm[:],
                                                             op0=ALU.add, op1=ALU.mult),
                     reads=["th", pname], writes=[dname])
            else:
                P.op("dve", lambda e: e.scalar_tensor_tensor(out=tmpf[:], in0=th[:], scalar=1.0, in1=pm[:],
                                                             op0=ALU.add, op1=ALU.mult),
                     reads=["th", pname], writes=["tmpf"])
                P.op("pool", lambda e: e.tensor_tensor(out=dst, in0=tmpf[:], in1=scale_t[:], op=ALU.mult),
                     reads=["tmpf", scale_name], writes=[dname])

        def load_x(dst, dname, src2d, t, scratch_name=None):
            P.dma("sp", lambda e: e.dma_start(out=dst[:], in_=src2d[t * 128:(t + 1) * 128, :]), dname,
                  reads=[scratch_name] if scratch_name else [], writes=[dname])

        def load_rope(dst, dname, t, rope2d=None):
            r2 = rope_d if rope2d is None else rope2d
            P.dma("sp", lambda e: e.dma_start(out=dst[:], in_=r2[t * 128:(t + 1) * 128, :]), dname, writes=[dname])

        J_F = dict(name="F", ntq=NT, xin=x_d[0], rope=rope_d, xs=xs_d, xsname="xs_d", out=out_d, bandc=bandc,
                   bcname="bandc", pa=0, kdst=kts_d[0], vdst=vs_d[0], kdname="kts_d", vdname="vs_d", half=False)
        if half:
            J_H = dict(name="H", ntq=NTL, xin=xh_d, rope=ropeh_d, xs=xsh_d, xsname="xsh_d", out=outh_d, bandc=bandch,
                       bcname="bandch", pa=1, kdst=kvl_k.ap(), vdst=kvl_v.ap(), kdname="kvl_k", vdname="kvl_v", half=True)
        else:
            J_H = dict(name="H", ntq=NT, xin=x_d[1], rope=rope_d, xs=xsh_d, xsname="xsh_d", out=outh_d, bandc=bandc,
                       bcname="bandc", pa=1, kdst=kts_d[1], vdst=vs_d[1], kdname="kts_d", vdname="vs_d", half=False)
        JOBS = [J_F, J_H]

        def make_bufset(tag, x, rope, hT, s1, r1, f, a, r, s8, r8, ks, vs_):
            return dict(tag=tag, x=(x, "x" + tag), rope=(rope, "rope" + tag), h=(hT, "hT" + tag), hT=(hT, "hT" + tag),
                        s1=(s1, "s1" + tag), r1=(r1, "r1" + tag), kf=(f, "kf" + tag), ka=(a, "ka" + tag),
                        kr=(r, "kr" + tag), s8=(s8, "s8" + tag), r8=(r8, "r8" + tag),
                        kst=(ks, "kst" + tag), vst=(vs_, "vst" + tag))

        def gen_prologue(kdst, kdname, vdst, vdname, t, B, store_eng="sp"):
            xsrc, xname = B["x"]
            rope_t, rname = B["rope"]
            hTb, hTn = B["hT"]
            kf_, kfn = B["kf"]
            kr_, krn = B["kr"]
            kst_, kstn = B["kst"]
            vst_, vstn = B["vst"]
            yield from gen_rmsnorm(xsrc, xname, hTb, hTn, B["h"][0], B["h"][1], B["s1"][0], B["s1"][1],
                                   B["r1"][0], B["r1"][1])
            yield
            pm, pname = proj(Wkv_full, "WinE", 1536, 256, hTb, hTn)
            P.op("dve", lambda e: e.tensor_copy(out=vst_[:], in_=pm[:, 128:256]), reads=[pname], writes=[vstn])
            P.op("dve", lambda e: e.tensor_copy(out=kf_[:, 0:128], in_=pm[:, 0:128]), reads=[pname], writes=[kfn])
            P.dma(store_eng, lambda e: e.dma_start(out=vdst[t * 128:(t + 1) * 128, :], in_=vst_[:]), vstn,
                  reads=[vstn], writes=[vdname + B["tag"]])
            yield
            yield from head_norm_rope((kf_, B["ka"][0]), (kfn, B["ka"][1]), 2, gk, "gk",
                                      rope_t, rname, kr_[:, 0:128].rearrange("p (h d) -> p h d", h=2), krn,
                                      B["s8"][0], B["r8"][0], B["s8"][1], B["r8"][1])
            yield
            yield
            P.op("pe", lambda e: e.transpose(out=psT[:, 0:128], in_=kr_[:, 0:128], identity=idb[:]),
                 reads=[krn, "idb"], writes=["psT"])
            P.op("dve", lambda e: e.tensor_copy(out=kst_[:], in_=psT[:, 0:128]), reads=["psT"], writes=[kstn])
            P.dma(store_eng, lambda e: e.dma_start(out=kdst[:, t * 128:(t + 1) * 128], in_=kst_[:]), kstn,
                  reads=[kstn], writes=[kdname + B["tag"]])
            yield

        BS_PRO = make_bufset("Q", xq, rpq, hT3, st1c, rs1c, kf, ka, kr, st8p, rs8p, kst, vst)
        BS_S1 = make_bufset("S", xs1, rp1, hT1, st1a, rs1a, qf, qa, qr, st8, rs8, kst2, vst2)
        P.alias.update({"xS": "xs1", "ropeS": "rp1", "hTS": "hT1", "s1S": "st1a", "r1S": "rs1a",
                        "kfS": "scr1", "kaS": "scr1", "krS": "qr", "s8S": "st8", "r8S": "rs8"})

        BS_P3 = make_bufset("P", xp, th[:, 0:128], hT2, st1, rs1, scr[:, 0:512], scr[:, 512:1024], mixT, den, rden,
                            diffT[:, 0:128], diffT[:, 128:256])
        P.alias.update({"xP": "xp", "ropeP": "th", "hTP": "hT2", "s1P": "st1", "r1P": "rs1", "kfP": "scr", "kaP": "scr",
                        "krP": "mixT", "s8P": "den", "r8P": "rden", "kstP": "diffT", "vstP": "diffT"})

        def gen_PRO(J, t):
            load_x(xq, "xQ", J["xs"], t, J["xsname"])
            load_rope(rpq, "ropeQ", t, J["rope"])
            yield
            yield from gen_prologue(J["kdst"], J["kdname"], J["vdst"], J["vdname"], t, BS_PRO)

        Wkv_full = WinE

        def gen_S1(J, t, src2d, sname, gi, nxt):
            NTq = J["ntq"]
            yield from gen_rmsnorm(xs1, "xs1", hT1, "hT1", hT1, "hT1", st1a, "st1a", rs1a, "rs1a")
            if nxt is not None:
                load_x(xs1, "xs1", nxt[2], nxt[1], nxt[3])
            yield
            pm, pname = proj(WinE, "WinE", 0, 512, hT1, "hT1")
            P.op("dve", lambda e, pm=pm: e.tensor_copy(out=pu[gi % 4][:], in_=pm[:]), reads=[pname], writes=[f"pu{gi % 4}"])
            yield
            pm, pname = proj(WinE, "WinE", 1024, 512, hT1, "hT1")
            P.op("dve", lambda e, pm=pm: e.tensor_copy(out=qf, in_=pm[:]), reads=[pname], writes=["qf"])
            yield
            qgen = head_norm_rope((qf, qa), ("qf", "qa"), 8, gq, "gq", rp1, "rp1",
                                  qr[:].rearrange("p (j g d) -> p g j d", j=4, g=2), "qr", st8, rs8, "st8", "rs8")
            next(qgen)
            yield
            pm, pname = proj(WinE, "WinE", 512, 512, hT1, "hT1")
            silu2(pm, pname, spz[gi % 3][:], f"spz{gi % 3}")
            yield
            next(qgen)
            yield
            pm, pname = proj(WinE, "WinE", 1792, 512, hT1, "hT1")
            silu2(pm, pname, saz[gi % 3][:], f"saz{gi % 3}")
            yield
            for _ in qgen:
                yield
            yield
            yield
            if nxt is not None:
                load_rope(rp1, "rp1", nxt[1], nxt[0]["rope"])
            s = gi % 2
            for j in range(4):
                P.op("pe", lambda e, j=j: e.transpose(out=psT[:, j * 128:(j + 1) * 128], in_=qr[:, j * 128:(j + 1) * 128],
                                                      identity=idb[:]),
                     reads=["qr", "idb"], writes=["psT"])
            P.op("dve", lambda e: e.tensor_copy(out=QTz[s][0][0:64, :], in_=psT[0:64, 0:512]), reads=["psT"],
                 writes=[f"QTz{s}"])
            P.op("dve", lambda e: e.tensor_copy(out=QTz[s][1][64:128, :], in_=psT[64:128, 0:512]), reads=["psT"],
                 writes=[f"QTz{s}"])
            yield

        def gen_ATT(J, t, gi):
            s = gi % 2
            NI = 2 * NT
            LAG = 2

            def qk(i):
                c, g = i // 2, i % 2
                b = i % 3
                P.op("pe", lambda e: e.matmul(psS[b][:], lhsT=KT[:, c * 128:(c + 1) * 128], rhs=QTz[s][g][:],
                                              start=True, stop=True),
                     reads=["KT", f"QTz{s}"], writes=[f"psS{b}"])
                P.op("act", lambda e: e.activation(out=pT[b][:], in_=psS[b][:], func=AF.Exp, scale=0.125),
                     reads=[f"psS{b}"], writes=[f"pT{b}"])

            def pv(i):
                c, g = i // 2, i % 2
                b = i % 3
                P.op("pe", lambda e: e.matmul(psO[g][0:VW, :], lhsT=V[:, c, g, :], rhs=pT[b][:],
                                              start=(c == 0), stop=(c == NT - 1)),
                     reads=["V", f"pT{b}"], writes=[f"psO{g}"])

            for i in range(NI + LAG):
                if i < NI:
                    qk(i)
                if i >= LAG:
                    pv(i - LAG)
                yield

        def gen_POST(J, t, src2d, sname, dst2d, dname_, gi):
            NTq = J["ntq"]
            bc_t, bc_n = J["bandc"], J["bcname"]
            load_x(xp, "xp", src2d, t, sname)
            for g in range(2):
                P.op("dve", lambda e, g=g: e.tensor_copy(out=oT[g][0:VW, :], in_=psO[g][0:VW, :]),
                     reads=[f"psO{g}"], writes=[f"oT{g}"])
            yield
            pm, pname = next_mm()
            var = 0 if t == 0 else (2 if t == NTq - 1 else 1)
            for g in range(4):
                hp = t > 0
                hn = t < NTq - 1
                P.op("pe", lambda e, g=g, pm=pm: e.matmul(pm[:, g * 128:(g + 1) * 128], lhsT=pu[gi % 4][:, g * 128:(g + 1) * 128],
                                                          rhs=bc_t[:, g * 3 + var, :], start=True, stop=not (hp or hn)),
                     reads=[f"pu{gi % 4}", bc_n], writes=[pname])
                if hp:
                    P.op("pe", lambda e, g=g, pm=pm: e.matmul(pm[:, g * 128:g * 128 + 8],
                                                              lhsT=pu[(gi - 1) % 4][:, g * 128:(g + 1) * 128],
                                                              rhs=bandp[:, g, :], start=False, stop=not hn),
                         reads=[f"pu{(gi - 1) % 4}", "bandp"], writes=[pname])
                if hn:
                    P.op("pe", lambda e, g=g, pm=pm: e.matmul(pm[:, g * 128 + 120:g * 128 + 128],
                                                              lhsT=pu[(gi + 1) % 4][:, g * 128:(g + 1) * 128],
                                                              rhs=bandn[:, g, :], start=False, stop=True),
                         reads=[f"pu{(gi + 1) % 4}", "bandn"], writes=[pname])
            P.op("dve", lambda e, pm=pm: e.tensor_copy(out=diffT[:], in_=pm[:]), reads=[pname], writes=["diffT"])
            yield
            pm, pname = next_mm()
            for g in range(4):
                P.op("pe", lambda e, g=g, pm=pm: e.matmul(pm[:, g * 128:(g + 1) * 128], lhsT=diffT[:, g * 128:(g + 1) * 128],
                                                          rhs=poolW[:, g, :], start=True, stop=True),
                     reads=["diffT", "poolW"], writes=[pname])
            P.op("dve", lambda e, pm=pm: e.tensor_tensor(out=mix[:, 0:512], in0=pm[:], in1=spz[gi % 3][:], op=ALU.mult),
                 reads=[pname, f"spz{gi % 3}"], writes=["mix"])
            yield
            for g in range(2):
                pm, pname = next_mm()
                for j in range(4):
                    P.op("pe", lambda e, g=g, j=j, pm=pm: e.transpose(out=pm[:, j * VW:(j + 1) * VW],
                                                                      in_=oT[g][0:VW, j * 128:(j + 1) * 128],
                                                                      identity=idf[0:VW, 0:VW]),
                         reads=[f"oT{g}", "idf"], writes=[pname])
                dview = pm[:, 0:4 * VW].rearrange("p (j w) -> p j w", w=VW)[:, :, 64]
                P.op("dve", lambda e, g=g, dview=dview: e.tensor_scalar(out=den[:, g * 4:(g + 1) * 4], in0=dview, scalar1=2.0,
                                                                        scalar2=None, op0=ALU.mult),
                     reads=[pname], writes=["den"])
                P.op("dve", lambda e, g=g: e.reciprocal(out=rden[:, g * 4:(g + 1) * 4], in_=den[:, g * 4:(g + 1) * 4]),
                     reads=["den"], writes=["rden"])
                for j in range(4):
                    hh = 4 * g + j
                    P.op("dve", lambda e, pm=pm, j=j, hh=hh: e.scalar_tensor_tensor(
                        out=mix[:, 512 + hh * 64:512 + (hh + 1) * 64], in0=pm[:, j * VW:j * VW + 64],
                        scalar=rden[:, hh:hh + 1], in1=saz[gi % 3][:, hh * 64:(hh + 1) * 64], op0=ALU.mult, op1=ALU.mult),
                        reads=[pname, "rden", f"saz{gi % 3}"], writes=["mix"])
                yield
            yield
            yield
            yield
            for k in range(8):
                P.op("pe", lambda e, k=k: e.transpose(out=psT[:, k * 128:(k + 1) * 128], in_=mix[:, k * 128:(k + 1) * 128],
                                                      identity=idb[:]),
                     reads=["mix", "idb"], writes=["psT"])
            P.op("dve", lambda e: e.tensor_copy(out=mixT[:], in_=psT[:]), reads=["psT"], writes=["mixT"])
            yield
            yield
            for sl in range(2):
                pm, pname = proj(WoutE, "WoutE", sl * 512, 512, mixT, "mixT")
                P.op("dve", lambda e, pm=pm, sl=sl: e.tensor_tensor(out=xp[:, sl * 512:(sl + 1) * 512], in0=pm[:],
                                                                    in1=xp[:, sl * 512:(sl + 1) * 512], op=ALU.add),
                     reads=[pname, "xp"], writes=["xp"])
                yield
            yield
            yield from gen_rmsnorm(xp, "xp", hT2, "hT2", hT2, "hT2", st1, "st1", rs1, "rs1", ny=3)
            yield
            for sl in range(2):
                pm, pname = proj(WinO, "WinO", sl * 512, 512, hT2, "hT2")
                P.op("dve", lambda e, pm=pm, sl=sl: e.tensor_copy(out=u[:, sl * 512:(sl + 1) * 512], in_=pm[:]),
                     reads=[pname], writes=["u"])
                yield
            for sl in range(2):
                pm, pname = proj(WinO, "WinO", 1024 + sl * 512, 512, hT2, "hT2")
                P.op("dve", lambda e, pm=pm, sl=sl: e.tensor_copy(out=vv[:, sl * 512:(sl + 1) * 512], in_=pm[:]),
                     reads=[pname], writes=["vv"])
                yield
            P.op("act", lambda e: e.activation(out=vv, in_=vv, func=AF.Gelu), reads=["vv"], writes=["vv"])
            P.op("act", lambda e: e.activation(out=u, in_=u, func=AF.Gelu), reads=["u"], writes=["u"])
            for sl in range(2):
                pm, pname = proj(WinO, "WinO", 2048 + sl * 512, 512, hT2, "hT2")
                silu2(pm, pname, tmpf[:], "tmpf")
                P.op("pool", lambda e, sl=sl: e.tensor_tensor(out=u[:, sl * 512:(sl + 1) * 512], in0=u[:, sl * 512:(sl + 1) * 512],
                                                              in1=tmpf[:], op=ALU.mult),
                     reads=["u", "tmpf"], writes=["u"])
                yield
            P.op("dve", lambda e: e.scalar_tensor_tensor(out=vvn[:], in0=vv, scalar=1.0, in1=vv, op0=ALU.mult,
                                                         op1=ALU.mult, accum_out=st1[:]),
                 reads=["vv"], writes=["vvn", "st1"])
            yield
            P.op("pool", lambda e: e.tensor_scalar(out=rs1[:], in0=st1[:], scalar1=1.0 / D, scalar2=EPS, op0=ALU.mult,
                                                   op1=ALU.add),
                 reads=["st1"], writes=["rs1"])
            P.op("pool", lambda e: e.tensor_tensor(out=rs1[:], in0=rs1[:], in1=mhalf[:, 0:1], op=ALU.pow),
                 reads=["rs1", "mhalf"], writes=["rs1"])
            yield
            P.op("dve", lambda e: e.scalar_tensor_tensor(out=vvn[:], in0=vv, scalar=rs1[:], in1=gS[:], op0=ALU.mult,
                                                         op1=ALU.mult),
                 reads=["vv", "rs1", "gS"], writes=["vvn"])
            yield
            yield
            for half in range(2):
                pm, pname = next_mm()
                for j in range(4):
                    hh = half * 4 + j
                    P.op("pe", lambda e, pm=pm, j=j, hh=hh: e.matmul(pm[:, j * 128:(j + 1) * 128], lhsT=WsT[:, hh, :],
                                                                    rhs=vvn[:, hh * 128:(hh + 1) * 128], start=True, stop=True),
                         reads=["WsT", "vvn"], writes=[pname])
                for j in range(4):
                    hh = half * 4 + j
                    P.op("dve", lambda e, pm=pm, j=j, hh=hh: e.scalar_tensor_tensor(
                        out=mix[:, hh * 128:(hh + 1) * 128], in0=pm[:, j * 128:(j + 1) * 128], scalar=bsT[:, hh:hh + 1],
                        in1=u[:, hh * 128:(hh + 1) * 128], op0=ALU.add, op1=ALU.mult),
                        reads=[pname, "bsT", "u"], writes=["mix"])
                yield
            yield
            yield
            for k in range(8):
                P.op("pe", lambda e, k=k: e.transpose(out=psT[:, k * 128:(k + 1) * 128], in_=mix[:, k * 128:(k + 1) * 128],
                                                      identity=idb[:]),
                     reads=["mix", "idb", "vvn"], writes=["psT"])
            P.op("dve", lambda e: e.tensor_copy(out=mixT[:], in_=psT[:]), reads=["psT"], writes=["mixT"])
            yield
            yield
            for sl in range(2):
                pm, pname = proj(WoutO, "WoutO", sl * 512, 512, mixT, "mixT")
                P.op("dve", lambda e, pm=pm, sl=sl: e.scalar_tensor_tensor(
                    out=xp[:, sl * 512:(sl + 1) * 512], in0=pm[:], scalar=0.5, in1=xp[:, sl * 512:(sl + 1) * 512],
                    op0=ALU.mult, op1=ALU.add),
                    reads=[pname, "xp"], writes=["xp"])
                yield
            P.dma("sp", lambda e: e.dma_start(out=dst2d[t * 128:(t + 1) * 128, :], in_=xp[:]), "stx",
                  reads=["xp"], writes=[dname_])

        def load_kv(J, layer_pair):
            P.op("pool", lambda e: e.memset(V[:], 1.0), writes=["V"])
            if J["half"] and layer_pair > 0:
                kg, vg = kvg_k.ap(), kvg_v.ap()
                o2 = (NTL - HT) * 128
                P.dma("sp", lambda e: [e.dma_start(out=KT[:, 0:HT * 128], in_=kg[0:128, 0:HT * 128]),
                                       e.dma_start(out=KT[:, HT * 128:L], in_=kg[128:256, o2:NTL * 128])],
                      "KT", reads=["kvg_k"], writes=["KT"], n=2)

                def vfn(e):
                    res = []
                    for g in range(2):
                        res.append(e.dma_start(out=V[:, 0:HT, g, 0:64],
                                               in_=vg[0:HT * 128, g * 64:(g + 1) * 64].rearrange("(c p) d -> p c d", p=128)))
                        res.append(e.dma_start(out=V[:, HT:NT, g, 0:64],
                                               in_=vg[NTL * 128 + o2:2 * NTL * 128, g * 64:(g + 1) * 64]
                                               .rearrange("(c p) d -> p c d", p=128)))
                    return res
                P.dma("sp", vfn, "V", reads=["kvg_v"], writes=["V"], n=4)
            else:
                ksrc = kts_d[J["pa"]]
                vsrc = vs_d[J["pa"]]
                P.dma("sp", lambda e: e.dma_start(out=KT[:], in_=ksrc), "KT", reads=["kts_d" + x_ for x_ in "QSP"],
                      writes=["KT"])
                P.dma("sp", lambda e: [e.dma_start(out=V[:, :, g, 0:64],
                                                   in_=vsrc[:, g * 64:(g + 1) * 64].rearrange("(c p) d -> p c d", p=128))
                                       for g in range(2)],
                      "V", reads=["vs_d" + x_ for x_ in "QSP"], writes=["V"], n=2)

        def count_steps(genfn):
            saved = (P.op, P.dma, mmi[0])
            P.op = lambda *a, **k: None
            P.dma = lambda *a, **k: None
            n = sum(1 for _ in genfn())
            P.op, P.dma = saved[0], saved[1]
            mmi[0] = saved[2]
            return n

        FRAC = 94

        def run_slot(gens):
            gens = [(g, n) for g, n in gens if g is not None]
            if not gens:
                return
            main, nmain = gens[0]
            others = gens[1:]
            done = [0] * len(others)
            for oi, (g, n) in enumerate(others):
                if next(g, "END") != "END":
                    done[oi] += 1
            for i in range(nmain):
                next(main, None)
                for oi, (g, n) in enumerate(others):
                    target = ((i + 1) * n * 100 + nmain * FRAC - 1) // (nmain * FRAC)
                    while done[oi] < min(target, n):
                        next(g, None)
                        done[oi] += 1
            for oi, (g, n) in enumerate(others):
                for _ in g:
                    pass
            for _ in main:
                pass

        npairs = depth // 2
        issue_pair_dmas(0)
        post_winE()
        bcastload(gk, "gk", k_norm[0:1, :])
        def gen_passA(seq, tiles, B):
            for t in tiles:
                load_x(B["x"][0], B["x"][1], x_d[seq], t)
                load_rope(B["rope"][0], B["rope"][1], t)
                yield from gen_prologue(kts_d[seq], "kts_d", vs_d[seq], "vs_d", t, B, store_eng="act")

        for seq in range(NSEQ):
            ga = gen_passA(seq, range(0, NT, 3), BS_PRO)
            gb = gen_passA(seq, range(1, NT, 3), BS_S1)
            gc_ = gen_passA(seq, range(2, NT, 3), BS_P3)
            for _ in range(8):
                next(ga, None)
            for _ in range(4):
                next(gb, None)
            alive = True
            while alive:
                alive = False
                for g in (ga, gb, gc_):
                    if next(g, "END") != "END":
                        alive = True
        post_rest()
        for j in range(npairs):
            last = (j == npairs - 1)
            if j > 0:
                issue_pair_dmas(j)
                post_winE()
                post_rest()
            if not last:
                load_kv_weights(j + 1)
            items = []
            for J in JOBS:
                src2d, sname = (J["xin"], None) if j == 0 else (J["xs"], J["xsname"])
                dst2d, dname_ = (J["out"], "out_" + J["name"]) if last else (J["xs"], J["xsname"])
                for t in range(J["ntq"]):
                    items.append((J, t, src2d, sname, dst2d, dname_))
            NI_ = len(items)
            I0 = items[0]
            load_x(xs1, "xs1", I0[2], I0[1], I0[3])
            load_rope(rp1, "rp1", I0[1], I0[0]["rope"])
            nS1 = count_steps(lambda: gen_S1(I0[0], 1, I0[2], I0[3], 1, None))
            nATT = count_steps(lambda: gen_ATT(I0[0], 1, 1))
            nPOST = count_steps(lambda: gen_POST(I0[0], 1, I0[2], I0[3], I0[4], I0[5], 1))
            nPRO = count_steps(lambda: gen_PRO(I0[0], 1))
            for k in range(-1, NI_ + 2):
                att = post = s1 = pro = (None, 0)
                if 0 <= k < NI_:
                    J, t = items[k][0], items[k][1]
                    if t == 0:
                        load_kv(J, j)
                    att = (gen_ATT(J, t, k), nATT)
                if 0 <= k - 1 < NI_:
                    J, t, a, b, c_, d_ = items[k - 1]
                    post = (gen_POST(J, t, a, b, c_, d_, k - 1), nPOST)
                if 0 <= k + 1 < NI_:
                    J, t, a, b, c_, d_ = items[k + 1]
                    nx = items[k + 2] if k + 2 < NI_ else None
                    s1 = (gen_S1(J, t, a, b, k + 1, None if nx is None else (nx[0], nx[1], nx[2], nx[3])), nS1)
                if (not last) and 0 <= k - 2 < NI_:
                    pro = (gen_PRO(items[k - 2][0], items[k - 2][1]), nPRO)
                if att[0] is not None:
                    run_slot([att, post, s1, pro])
                else:
                    run_slot([post, s1, pro])
            if half and not last:
                groups = [[c, c + 4] for c in range(4)]
                P.cc(lambda e: e.collective_compute("AllGather", ALU.bypass, replica_groups=groups,
                                                    ins=[kvl_k.ap().opt()], outs=[kvg_k.ap().opt()]),
                     "cck", reads=["kvl_k" + x_ for x_ in "QSP"], writes=["kvg_k"])
                P.cc(lambda e: e.collective_compute("AllGather", ALU.bypass, replica_groups=groups,
                                                    ins=[kvl_v.ap().opt()], outs=[kvg_v.ap().opt()]),
                     "ccv", reads=["kvl_v" + x_ for x_ in "QSP"], writes=["kvg_v"])
        P.emit()
    return nc


_CACHE = {}


_BANDS = band_tables()


def _consts(L):
    return {
        "ident_b": np.eye(128, dtype=np.float32).astype(ml_dtypes.bfloat16),
        "ident_f": np.eye(128, dtype=np.float32),
        "rope": rope_table(L),
        "band_c": _BANDS[0], "band_p": _BANDS[1], "band_n": _BANDS[2],
    }


def run_step(x_prompt, x_sample, weights, depth=4):
    L = x_prompt.shape[1]
    NT = L // 128
    NTL = half_tiles(NT)
    HT = NT // 2
    key = (L, depth)
    if key not in _CACHE:
        _CACHE[key] = build_program(L, 2, depth, half=True)
    nc = _CACHE[key]
    consts = _consts(L)
    bc = consts["band_c"]
    in_maps = []
    for c in range(N_CORES):
        lower = c < 4
        g0 = 0 if lower else NT - NTL
        xs_ = x_sample[c % 4]
        m = {"x": np.ascontiguousarray(np.stack([x_prompt[c], xs_], axis=0), dtype=np.float32),
             "xh": np.ascontiguousarray(xs_[g0 * 128:(g0 + NTL) * 128], dtype=np.float32),
             "rope_h": np.ascontiguousarray(consts["rope"][g0 * 128:(g0 + NTL) * 128])}
        sel = [0, 1, 1] if lower else [1, 1, 2]
        m["band_ch"] = np.ascontiguousarray(bc[:, [g * 3 + v for g in range(4) for v in sel], :])
        for k, v in weights.items():
            m[k] = np.ascontiguousarray(v, dtype=np.float32)
        m.update(consts)
        in_maps.append(m)
    res = run_bass_kernel_spmd(nc, in_maps, core_ids=list(range(N_CORES)))
    outs = res.results
    y_prompt = np.stack([np.asarray(outs[c]["out"]) for c in range(N_CORES)], axis=0).astype(np.float32)
    y_sample = np.stack([np.concatenate([np.asarray(outs[s_]["out_h"])[0:HT * 128],
                                         np.asarray(outs[s_ + 4]["out_h"])[(NTL - HT) * 128:NTL * 128]], axis=0)
                         for s_ in range(4)], axis=0).astype(np.float32)
    return y_prompt, y_sample


def kernel(x_prompt, x_sample, norm_e, w_in_e, pool_w, pool_scale, q_norm, k_norm, w_out_e,
           norm_o, w_in_o, sgu_norm, w_s, b_s, w_out_o):
    x_prompt = np.asarray(x_prompt)
    x_sample = np.asarray(x_sample)
    weights = dict(norm_e=norm_e, w_in_e=w_in_e, pool_w=pool_w, pool_scale=pool_scale, q_norm=q_norm,
                   k_norm=k_norm, w_out_e=w_out_e, norm_o=norm_o, w_in_o=w_in_o, sgu_norm=sgu_norm,
                   w_s=w_s, b_s=b_s, w_out_o=w_out_o)
    weights = {k: np.asarray(v) for k, v in weights.items()}
    return run_step(x_prompt, x_sample, weights)
```

```python
import contextlib
import numpy as np
import ml_dtypes
import concourse.bass as bass
import concourse.mybir as mybir
from concourse.bass_utils import run_bass_kernel_spmd

F32 = mybir.dt.float32
BF16 = mybir.dt.bfloat16
AF = mybir.ActivationFunctionType
ALU = mybir.AluOpType
AX = mybir.AxisListType

ENGS = ("pe", "act", "dve", "pool", "sp")

D = 1024
EPS = 1e-6
GRID_W = 64
POOL_WINDOWS = (2, 4, 8, 16)
VW = 66
N_CORES = 8


class Prog:
    def __init__(self, nc):
        self.nc = nc
        self.streams = {e: [] for e in ENGS}
        self.count = {e: 0 for e in ENGS}
        self.seen = {e: {} for e in ENGS}
        self.last_w = {}
        self.readers = {}
        self.dma_count = {}
        self.sem_keys = []
        self.alias = {}

    def _semkey(self, k):
        if k not in self.sem_keys:
            self.sem_keys.append(k)
        return k

    def _deps(self, eng, reads, writes):
        evs = {}
        reads = [self.alias.get(r, r) for r in reads]
        writes = [self.alias.get(w, w) for w in writes]

        def add(ev):
            if ev is None:
                return
            k, v, e = ev
            if e == "pe" and eng == "pe":
                return
            if v > evs.get(k, 0):
                evs[k] = v

        for r in reads:
            add(self.last_w.get(r))
        for w in writes:
            add(self.last_w.get(w))
            for k, (v, e) in self.readers.get(w, {}).items():
                add((k, v, e))
        out = []
        seen = self.seen[eng]
        for k, v in evs.items():
            if seen.get(k, 0) >= v:
                continue
            seen[k] = v
            out.append((k, v))
        return out

    def _record(self, ev, reads, writes):
        k, v, e = ev
        reads = [self.alias.get(r, r) for r in reads]
        writes = [self.alias.get(w, w) for w in writes]
        for r in reads:
            self.readers.setdefault(r, {})[k] = (v, e)
        for w in writes:
            self.last_w[w] = ev
            self.readers[w] = {}

    def op(self, eng, fn, reads=(), writes=()):
        waits = self._deps(eng, reads, writes)
        self.count[eng] += 1
        k = self._semkey("e:" + eng)
        ev = (k, self.count[eng], eng)
        self.streams[eng].append((waits, fn, (k, 1)))
        self._record(ev, reads, writes)
        return ev

    def dma(self, eng, fn, key, reads=(), writes=(), n=1):
        waits = self._deps(eng, reads, writes)
        k = self._semkey("d:" + key)
        self.dma_count[k] = self.dma_count.get(k, 0) + 16 * n
        ev = (k, self.dma_count[k], "dma")
        self.streams[eng].append((waits, fn, (k, 16)))
        self._record(ev, reads, writes)
        return ev

    def cc(self, fn, key, reads=(), writes=()):
        waits = self._deps("pool", reads, writes)
        k = self._semkey("c:" + key)
        self.dma_count[k] = self.dma_count.get(k, 0) + 1
        ev = (k, self.dma_count[k], "dma")
        self.streams["pool"].append((waits, fn, (k, 1)))
        self._record(ev, reads, writes)
        return ev

    def emit(self, final_wait_eng="sp"):
        nc = self.nc
        with contextlib.ExitStack() as st:
            sems = {}
            for k in self.sem_keys:
                sems[k] = st.enter_context(nc.semaphore(k.replace(":", "_")))
            block = st.enter_context(nc.Block())
            finals = list(self.dma_count.items())

            def run(engine, name):
                for waits, fn, (k, inc) in self.streams[name]:
                    for wk, wv in waits:
                        engine.wait_ge(sems[wk], wv)
                    res = fn(engine)
                    if not isinstance(res, (list, tuple)):
                        res = [res]
                    for ins in res:
                        ins.then_inc(sems[k], inc)
                if name == final_wait_eng:
                    for fk, fv in finals:
                        engine.wait_ge(sems[fk], fv)
                    for e2 in ("pe", "act", "dve", "pool"):
                        kk = "e:" + e2
                        if kk in sems and self.count[e2] > 0:
                            engine.wait_ge(sems[kk], self.count[e2])

            @block.tensor
            def _(e):
                run(e, "pe")

            @block.scalar
            def _(e):
                run(e, "act")

            @block.vector
            def _(e):
                run(e, "dve")

            @block.gpsimd
            def _(e):
                run(e, "pool")

            @block.sync
            def _(e):
                run(e, "sp")


def rope_table(L):
    t = np.arange(L)
    row = (t // GRID_W).astype(np.float32)
    col = (t % GRID_W).astype(np.float32)
    inv = (1.0 / (np.float32(10000.0) ** (np.arange(16, dtype=np.float32) / np.float32(16)))).astype(np.float32)
    ar = (row[:, None] * inv[None, :]).astype(np.float32)
    ac = (col[:, None] * inv[None, :]).astype(np.float32)
    cr, sr, cc, sc = np.cos(ar), np.sin(ar), np.cos(ac), np.sin(ac)
    C = np.concatenate([cr, cr, cc, cc], axis=1)
    S = np.concatenate([-sr, sr, -sc, sc], axis=1)
    return np.ascontiguousarray(np.concatenate([C, S], axis=1).astype(np.float32))


def band_tables():
    bc = np.zeros((128, 12, 128), np.float32)
    bp = np.zeros((128, 4, 8), np.float32)
    bn = np.zeros((128, 4, 8), np.float32)
    for g, w in enumerate(POOL_WINDOWS):
        hw = w // 2
        for var in range(3):
            for t in range(128):
                if var == 0:
                    lo, hi = max(t - hw, 0), t + hw
                elif var == 1:
                    lo, hi = t - hw, t + hw
                else:
                    lo, hi = t - hw, min(t + hw, 128)
                cnt = hi - lo
                for s_ in range(max(lo, 0), min(hi, 128)):
                    bc[s_, g * 3 + var, t] += 1.0 / cnt
                bc[t, g * 3 + var, t] -= 1.0
        for t in range(128):
            lo, hi = t - hw, t + hw
            for s_ in range(lo, 0):
                bp[128 + s_, g, t] += 1.0 / w
            for s_ in range(128, hi):
                bn[s_ - 128, g, t - 120] += 1.0 / w
    bf = ml_dtypes.bfloat16
    return bc.astype(bf), bp.astype(bf), bn.astype(bf)


def half_tiles(NT):
    return NT // 2 + 2


def build_program(L, NSEQ, depth=4, half=True):
    NT = L // 128
    assert NT >= 4 and NSEQ == 2
    NTL = half_tiles(NT) if half else NT
    HT = NT // 2
    nc = bass.Bass("TRN2", target_bir_lowering=False)

    def din(name, shape, dt=F32):
        return nc.dram_tensor(name, list(shape), dt, kind="ExternalInput").ap()

    x_d = din("x", [NSEQ, L, D])
    norm_e = din("norm_e", [2, D])
    w_in_e = din("w_in_e", [2, D, 2304])
    pool_w = din("pool_w", [2, 4, 128, 128])
    pool_scale = din("pool_scale", [2, 512])
    q_norm = din("q_norm", [2, 64])
    k_norm = din("k_norm", [2, 64])
    w_out_e = din("w_out_e", [2, D, D])
    norm_o = din("norm_o", [2, D])
    w_in_o = din("w_in_o", [2, D, 3072])
    sgu_norm = din("sgu_norm", [2, D])
    w_s = din("w_s", [2, 8, 128, 128])
    b_s = din("b_s", [2, 8, 128])
    w_out_o = din("w_out_o", [2, D, D])
    ident_b = din("ident_b", [128, 128], BF16)
    ident_f = din("ident_f", [128, 128], F32)
    rope_d = din("rope", [L, 128], F32)
    bc_d = din("band_c", [128, 12, 128], BF16)
    bp_d = din("band_p", [128, 4, 8], BF16)
    bn_d = din("band_n", [128, 4, 8], BF16)
    xh_d = din("xh", [NTL * 128, D])
    ropeh_d = din("rope_h", [NTL * 128, 128], F32)
    bch_d = din("band_ch", [128, 12, 128], BF16)

    out_d = nc.dram_tensor("out", [L, D], F32, kind="ExternalOutput").ap()
    outh_d = nc.dram_tensor("out_h", [NTL * 128, D], F32, kind="ExternalOutput").ap()
    xs_d = nc.dram_tensor("xs", [L, D], F32, kind="Internal").ap()
    xsh_d = nc.dram_tensor("xsh", [NTL * 128, D], F32, kind="Internal").ap()
    kvl_k = nc.dram_tensor("kvl_k", [128, NTL * 128], BF16, kind="Internal")
    kvl_v = nc.dram_tensor("kvl_v", [NTL * 128, 128], BF16, kind="Internal")
    kvg_k = nc.dram_tensor("kvg_k", [2 * 128, NTL * 128], BF16, kind="Internal")
    kvg_v = nc.dram_tensor("kvg_v", [2 * NTL * 128, 128], BF16, kind="Internal")
    kts_d = nc.dram_tensor("kts", [NSEQ, 128, L], BF16, kind="Internal").ap()
    vs_d = nc.dram_tensor("vs", [NSEQ, L, 128], BF16, kind="Internal").ap()

    with contextlib.ExitStack() as st:
        def sb(name, shape, dt):
            return st.enter_context(nc.sbuf_tensor("s_" + name, list(shape), dt))

        def ps(name, shape, dt):
            return st.enter_context(nc.psum_tensor("p_" + name, list(shape), dt))

        WinE = sb("WinE", [128, 8, 2304], BF16)
        WoutE = sb("WoutE", [128, 8, 1024], BF16)
        poolW = sb("poolW", [128, 4, 128], BF16)
        WinO = sb("WinO", [128, 8, 3072], BF16)
        WoutO = sb("WoutO", [128, 8, 1024], BF16)
        WsT = sb("WsT", [128, 8, 128], BF16)
        gS = sb("gS", [128, D], F32)
        gcE = sb("gcE", [128, 8], F32)
        gcO = sb("gcO", [128, 8], F32)
        gcEn = sb("gcEn", [128, 8], F32)
        gq = sb("gq", [128, 64], F32)
        gk = sb("gk", [128, 64], F32)
        bsT = sb("bsT", [128, 8], F32)
        idb = sb("idb", [128, 128], BF16)
        idf = sb("idf", [128, VW], F32)
        bandc = sb("bandc", [128, 12, 128], BF16)
        bandch = sb("bandch", [128, 12, 128], BF16)
        bandp = sb("bandp", [128, 4, 8], BF16)
        bandn = sb("bandn", [128, 4, 8], BF16)
        mhalf = sb("mhalf", [128, 8], F32)
        KT = sb("KT", [128, L], BF16)
        V = sb("V", [128, NT, 2, VW], BF16)
        xs1 = sb("xs1", [128, D], F32)
        rp1 = sb("rp1", [128, 128], F32)
        hT1 = sb("hT1", [128, D], BF16)
        scr1 = sb("scr1", [128, D], F32)
        qf = scr1[:, 0:512]
        qa = scr1[:, 512:1024]
        qr = sb("qr", [128, 512], BF16)
        st8 = sb("st8", [128, 8], F32)
        rs8 = sb("rs8", [128, 8], F32)
        pu = [sb(f"pu{i}", [128, 512], BF16) for i in range(4)]
        spz = [sb(f"spz{i}", [128, 512], BF16) for i in range(3)]
        saz = [sb(f"saz{i}", [128, 512], BF16) for i in range(3)]
        QTz = [[sb(f"QTz{i}{g}", [128, 512], BF16) for g in range(2)] for i in range(2)]
        th = sb("th", [128, 512], F32)
        tmpf = sb("tmpf", [128, 512], F32)
        st1a = sb("st1a", [128, 1], F32)
        rs1a = sb("rs1a", [128, 1], F32)
        st1 = sb("st1", [128, 1], F32)
        rs1 = sb("rs1", [128, 1], F32)
        pT = [sb(f"pT{i}", [128, 512], BF16) for i in range(3)]
        xp = sb("xp", [128, D], F32)
        hT2 = sb("hT2", [128, D], BF16)
        scr = sb("scr", [128, D], F32)
        scr2 = sb("scr2", [128, D], F32)
        vv = scr[:, :]
        u = scr2[:, :]
        oT = [scr2[:, i * 512:(i + 1) * 512] for i in range(2)]
        den = sb("den", [128, 8], F32)
        rden = sb("rden", [128, 8], F32)
        diffT = sb("diffT", [128, 512], BF16)
        mix = sb("mix", [128, D], BF16)
        mixT = sb("mixT", [128, D], BF16)
        vvn = mixT
        kf = sb("kf", [128, 128], F32)
        ka = sb("ka", [128, 128], F32)
        kr = sb("kr", [128, 128], BF16)
        st8p = sb("st8p", [128, 8], F32)
        rs8p = sb("rs8p", [128, 8], F32)
        kst = sb("kst", [128, 128], BF16)
        vst = sb("vst", [128, 128], BF16)
        kst2 = sb("kst2", [128, 128], BF16)
        vst2 = sb("vst2", [128, 128], BF16)
        xq = sb("xq", [128, D], F32)
        rpq = sb("rpq", [128, 128], F32)
        hT3 = sb("hT3", [128, D], BF16)
        st1c = sb("st1c", [128, 1], F32)
        rs1c = sb("rs1c", [128, 1], F32)
        psT = ps("psT", [128, 1024], BF16)
        psMM = [ps(f"psMM{i}", [128, 512], F32) for i in range(2)]
        psS = [ps(f"psS{i}", [128, 512], F32) for i in range(3)]
        psO = [ps(f"psO{i}", [128, 512], F32) for i in range(2)]

        P = Prog(nc)
        P.alias = {"qf": "scr1", "qa": "scr1", "vv": "scr", "oT0": "scr2", "oT1": "scr2", "u": "scr2",
                   "vvn": "mixT", "wsn": "mix", "Wkv": "WinE"}
        wsn = mix[:, :].rearrange("p (h q) -> p h q", h=8)
        Wkv = WinE[:, :, 1536:1792]
        mmi = [0]

        def next_mm():
            i = mmi[0] % 2
            mmi[0] += 1
            return psMM[i], f"psMM{i}"

        def castload(dst, dst_name, src, ncols, chunk=1024):
            srcv = src.rearrange("(k p) n -> p k n", p=128)
            chunks = [(c, min(c + chunk, ncols)) for c in range(0, ncols, chunk)]

            def fn(e):
                return [e.dma_start(out=dst[:, :, a:b], in_=srcv[:, :, a:b]) for a, b in chunks]
            P.dma("pool", fn, dst_name, writes=[dst_name], n=len(chunks))

        def bcastload(dst, dst_name, src_row, eng="sp"):
            P.dma(eng, lambda e: e.dma_start(out=dst[:], in_=src_row.partition_broadcast(128)), dst_name,
                  writes=[dst_name])

        def colload(dst, dst_name, src_vec):
            P.dma("sp", lambda e: e.dma_start(out=dst[:], in_=src_vec.rearrange("(k p) -> p k", p=128),
                                              allow_slow_non_contiguous=True), dst_name, writes=[dst_name])

        def rowscale(W, wname, gc, gname, ncols):
            for k in range(8):
                P.op("dve", lambda e, k=k: e.tensor_scalar(out=W[:, k, 0:ncols], in0=W[:, k, 0:ncols],
                                                           scalar1=gc[:, k:k + 1], scalar2=None, op0=ALU.mult),
                     reads=[wname, gname], writes=[wname])

        P.dma("sp", lambda e: e.dma_start(out=idb[:], in_=ident_b), "idb", writes=["idb"])
        P.dma("sp", lambda e: e.dma_start(out=idf[:], in_=ident_f[:, 0:VW]), "idf", writes=["idf"])
        P.dma("sp", lambda e: e.dma_start(out=bandc[:], in_=bc_d), "bandc", writes=["bandc"])
        P.dma("sp", lambda e: e.dma_start(out=bandch[:], in_=bch_d), "bandch", writes=["bandch"])
        P.dma("sp", lambda e: e.dma_start(out=bandp[:], in_=bp_d), "bandp", writes=["bandp"])
        P.dma("sp", lambda e: e.dma_start(out=bandn[:], in_=bn_d), "bandn", writes=["bandn"])
        P.op("pool", lambda e: e.memset(mhalf[:], -0.5), writes=["mhalf"])
        for i in range(2):
            for g in range(2):
                P.op("pool", lambda e, i=i, g=g: e.memset(QTz[i][g][:], 0.0), writes=[f"QTz{i}"])

        def load_kv_weights(j):
            def fn(e):
                return [e.dma_start(out=Wkv, in_=w_in_e[j][:, 1536:1792].rearrange("(k p) n -> p k n", p=128))]
            P.dma("pool", fn, "WinE", writes=["WinE"])
            colload(gcEn, "gcEn", norm_e[j])
            for k in range(8):
                P.op("dve", lambda e, k=k: e.tensor_scalar(out=WinE[:, k, 1536:1792], in0=WinE[:, k, 1536:1792],
                                                           scalar1=gcEn[:, k:k + 1], scalar2=None, op0=ALU.mult),
                     reads=["WinE", "gcEn"], writes=["WinE"])
            bcastload(gk, "gk", k_norm[j:j + 1, :])

        def issue_pair_dmas(j):
            castload(WinE, "WinE", w_in_e[j], 2304)
            colload(gcE, "gcE", norm_e[j])
            bcastload(gq, "gq", q_norm[j:j + 1, :])
            bcastload(tmpf, "tmpf", pool_scale[j:j + 1, :])
            P.dma("pool", lambda e: e.dma_start(out=poolW[:], in_=pool_w[j].rearrange("g c d -> c g d")), "poolW",
                  writes=["poolW"])
            castload(WoutE, "WoutE", w_out_e[j], 1024)
            castload(WinO, "WinO", w_in_o[j], 3072)
            colload(gcO, "gcO", norm_o[j])
            bcastload(gS, "gS", sgu_norm[j:j + 1, :])
            P.dma("sp", lambda e: e.dma_start(out=bsT[:], in_=b_s[j].rearrange("h p -> p h"),
                                              allow_slow_non_contiguous=True), "bsT", writes=["bsT"])
            P.dma("pool", lambda e: e.dma_start(out=wsn, in_=w_s[j].rearrange("h p q -> p h q")), "wsn",
                  writes=["wsn"])
            castload(WoutO, "WoutO", w_out_o[j], 1024)

        def post_winE():
            rowscale(WinE, "WinE", gcE, "gcE", 2304)

        def post_rest():
            P.op("dve", lambda e: e.tensor_scalar(out=tmpf[:], in0=tmpf[:], scalar1=0.5, scalar2=None, op0=ALU.mult),
                 reads=["tmpf"], writes=["tmpf"])
            P.op("dve", lambda e: e.tensor_tensor(out=poolW[:].rearrange("c g d -> c (g d)"),
                                                  in0=poolW[:].rearrange("c g d -> c (g d)"), in1=tmpf[:], op=ALU.mult),
                 reads=["poolW", "tmpf"], writes=["poolW"])
            rowscale(WinO, "WinO", gcO, "gcO", 3072)
            for hh in range(8):
                P.op("pe", lambda e, hh=hh: e.transpose(out=psT[:, hh * 128:(hh + 1) * 128], in_=wsn[:, hh, :],
                                                        identity=idb[:]),
                     reads=["wsn", "idb"], writes=["psT"])
            P.op("dve", lambda e: e.tensor_copy(out=WsT[:].rearrange("p h q -> p (h q)"), in_=psT[:]),
                 reads=["psT"], writes=["WsT"])

        def gen_rmsnorm(xsrc, xname, hTdst, hTname, hb, hbn, s1, s1n, r1, r1n, ny=2):
            P.op("dve", lambda e: e.scalar_tensor_tensor(out=hb[:], in0=xsrc[:], scalar=1.0, in1=xsrc[:],
                                                         op0=ALU.mult, op1=ALU.mult, accum_out=s1[:]),
                 reads=[xname], writes=[hbn, s1n])
            for _ in range(ny):
                yield
            P.op("dve", lambda e: e.tensor_scalar(out=r1[:], in0=s1[:], scalar1=1.0 / D, scalar2=EPS,
                                                  op0=ALU.mult, op1=ALU.add),
                 reads=[s1n], writes=[r1n])
            P.op("pool", lambda e: e.tensor_tensor(out=r1[:], in0=r1[:], in1=mhalf[:, 0:1], op=ALU.pow),
                 reads=[r1n, "mhalf"], writes=[r1n])
            for _ in range(ny):
                yield
            P.op("dve", lambda e: e.tensor_scalar(out=hb[:], in0=xsrc[:], scalar1=r1[:], scalar2=None, op0=ALU.mult),
                 reads=[xname, r1n], writes=[hbn])
            for _ in range(ny):
                yield
            for k in range(8):
                P.op("pe", lambda e, k=k: e.transpose(out=psT[:, k * 128:(k + 1) * 128], in_=hb[:, k * 128:(k + 1) * 128],
                                                      identity=idb[:]),
                     reads=[hbn, "idb"], writes=["psT"])
            P.op("dve", lambda e: e.tensor_copy(out=hTdst[:], in_=psT[:]), reads=["psT"], writes=[hTname])
            yield

        def proj(W, wname, c0, ncols, lhs, lname):
            pm, pname = next_mm()
            for k in range(8):
                P.op("pe", lambda e, k=k, pm=pm: e.matmul(pm[:, 0:ncols], lhsT=lhs[:, k * 128:(k + 1) * 128],
                                                          rhs=W[:, k, c0:c0 + ncols], start=(k == 0), stop=(k == 7)),
                     reads=[lname, wname], writes=[pname])
            return pm, pname

        def head_norm_rope(bufs, names, nh, gain, gname, rope_t, rname, out_view, dname, s8, r8, s8n, r8n):
            src, ta = bufs
            sname, aname = names
            n = nh * 64
            s3 = src[:, 0:n].rearrange("p (h d) -> p h d", h=nh)
            a3 = ta[:, 0:n].rearrange("p (h d) -> p h d", h=nh)
            ceng0 = "dve" if nh == 2 else "pool"
            P.op(ceng0, lambda e: e.tensor_tensor(out=ta[:, 0:n], in0=src[:, 0:n], in1=src[:, 0:n], op=ALU.mult),
                 reads=[sname], writes=[aname])
            yield
            P.op("dve", lambda e: e.tensor_reduce(out=s8[:, 0:nh], in_=a3, axis=AX.X, op=ALU.add),
                 reads=[aname], writes=[s8n])
            P.op(ceng0, lambda e: e.tensor_scalar(out=r8[:, 0:nh], in0=s8[:, 0:nh], scalar1=1.0 / 64, scalar2=EPS,
                                                  op0=ALU.mult, op1=ALU.add),
                 reads=[s8n], writes=[r8n])
            P.op("pool", lambda e: e.tensor_tensor(out=r8[:, 0:nh], in0=r8[:, 0:nh], in1=mhalf[:, 0:nh], op=ALU.pow),
                 reads=[r8n, "mhalf"], writes=[r8n])
            yield
            P.op("dve", lambda e: e.tensor_tensor(out=s3, in0=s3, in1=r8[:, 0:nh].unsqueeze(2).to_broadcast([128, nh, 64]),
                                                  op=ALU.mult),
                 reads=[sname, r8n], writes=[sname])
            yield
            ceng = "dve" if nh == 2 else "pool"
            P.op(ceng, lambda e: e.tensor_tensor(out=s3, in0=s3, in1=gain[:].unsqueeze(1).to_broadcast([128, nh, 64]),
                                                 op=ALU.mult),
                 reads=[sname, gname], writes=[sname])
            yield
            a5 = ta[:, 0:n].rearrange("p (h a b c) -> p h a b c", h=nh, a=2, b=2)
            s5 = src[:, 0:n].rearrange("p (h a b c) -> p h a b c", h=nh, a=2, b=2)
            S5 = rope_t[:, 64:128].rearrange("p (a b c) -> p a b c", a=2, b=2)
            for blk in range(2):
                P.op(ceng, lambda e, blk=blk: e.tensor_tensor(
                    out=a5[:, :, :, blk, :], in0=s5[:, :, :, 1 - blk, :],
                    in1=S5[:, :, blk, :].unsqueeze(1).to_broadcast([128, nh, 2, 16]), op=ALU.mult),
                    reads=[sname, rname], writes=[aname])
            P.op("dve", lambda e: e.tensor_tensor(out=s3, in0=s3,
                                                  in1=rope_t[:, 0:64].unsqueeze(1).to_broadcast([128, nh, 64]), op=ALU.mult),
                 reads=[sname, rname], writes=[sname])
            yield
            if nh == 8:
                fa = ta[:, 0:n].rearrange("p (g j d) -> p g j d", g=2, j=4)
                fs = src[:, 0:n].rearrange("p (g j d) -> p g j d", g=2, j=4)
            else:
                fa, fs = a3, s3
            P.op("dve", lambda e: e.tensor_tensor(out=out_view, in0=fa, in1=fs, op=ALU.add),
                 reads=[aname, sname], writes=[dname])

        def silu2(pm, pname, dst, dname, scale_t=None, scale_name=None):
            P.op("act", lambda e: e.activation(out=th[:], in_=pm[:], func=AF.Tanh, scale=0.5),
                 reads=[pname], writes=["th"])
            if scale_t is None:
                P.op("dve", lambda e: e.scalar_tensor_tensor(out=dst, in0=th[:], scalar=1.0, in1=pm[:],
                                                             op0=ALU.add, op1=ALU.mult),
                     reads=["th", pname], writes=[dname])
            else:
                P.op("dve", lambda e: e.scalar_tensor_tensor(out=tmpf[:], in0=th[:], scalar=1.0, in1=pm[:],
                                                             op0=ALU.add, op1=ALU.mult),
                     reads=["th", pname], writes=["tmpf"])
                P.op("pool", lambda e: e.tensor_tensor(out=dst, in0=tmpf[:], in1=scale_t[:], op=ALU.mult),
                     reads=["tmpf", scale_name], writes=[dname])

        def load_x(dst, dname, src2d, t, scratch_name=None):
            P.dma("sp", lambda e: e.dma_start(out=dst[:], in_=src2d[t * 128:(t + 1) * 128, :]), dname,
                  reads=[scratch_name] if scratch_name else [], writes=[dname])

        def load_rope(dst, dname, t, rope2d=None):
            r2 = rope_d if rope2d is None else rope2d
            P.dma("sp", lambda e: e.dma_start(out=dst[:], in_=r2[t * 128:(t + 1) * 128, :]), dname, writes=[dname])

        J_F = dict(name="F", ntq=NT, xin=x_d[0], rope=rope_d, xs=xs_d, xsname="xs_d", out=out_d, bandc=bandc,
                   bcname="bandc", pa=0, kdst=kts_d[0], vdst=vs_d[0], kdname="kts_d", vdname="vs_d", half=False)
        if half:
            J_H = dict(name="H", ntq=NTL, xin=xh_d, rope=ropeh_d, xs=xsh_d, xsname="xsh_d", out=outh_d, bandc=bandch,
                       bcname="bandch", pa=1, kdst=kvl_k.ap(), vdst=kvl_v.ap(), kdname="kvl_k", vdname="kvl_v", half=True)
        else:
            J_H = dict(name="H", ntq=NT, xin=x_d[1], rope=rope_d, xs=xsh_d, xsname="xsh_d", out=outh_d, bandc=bandc,
                       bcname="bandc", pa=1, kdst=kts_d[1], vdst=vs_d[1], kdname="kts_d", vdname="vs_d", half=False)
        JOBS = [J_F, J_H]

        def make_bufset(tag, x, rope, hT, s1, r1, f, a, r, s8, r8, ks, vs_):
            return dict(tag=tag, x=(x, "x" + tag), rope=(rope, "rope" + tag), h=(hT, "hT" + tag), hT=(hT, "hT" + tag),
                        s1=(s1, "s1" + tag), r1=(r1, "r1" + tag), kf=(f, "kf" + tag), ka=(a, "ka" + tag),
                        kr=(r, "kr" + tag), s8=(s8, "s8" + tag), r8=(r8, "r8" + tag),
                        kst=(ks, "kst" + tag), vst=(vs_, "vst" + tag))

        def gen_prologue(kdst, kdname, vdst, vdname, t, B, store_eng="sp"):
            xsrc, xname = B["x"]
            rope_t, rname = B["rope"]
            hTb, hTn = B["hT"]
            kf_, kfn = B["kf"]
            kr_, krn = B["kr"]
            kst_, kstn = B["kst"]
            vst_, vstn = B["vst"]
            yield from gen_rmsnorm(xsrc, xname, hTb, hTn, B["h"][0], B["h"][1], B["s1"][0], B["s1"][1],
                                   B["r1"][0], B["r1"][1])
            yield
            pm, pname = proj(Wkv_full, "WinE", 1536, 256, hTb, hTn)
            P.op("dve", lambda e: e.tensor_copy(out=vst_[:], in_=pm[:, 128:256]), reads=[pname], writes=[vstn])
            P.op("dve", lambda e: e.tensor_copy(out=kf_[:, 0:128], in_=pm[:, 0:128]), reads=[pname], writes=[kfn])
            P.dma(store_eng, lambda e: e.dma_start(out=vdst[t * 128:(t + 1) * 128, :], in_=vst_[:]), vstn,
                  reads=[vstn], writes=[vdname + B["tag"]])
            yield
            yield from head_norm_rope((kf_, B["ka"][0]), (kfn, B["ka"][1]), 2, gk, "gk",
                                      rope_t, rname, kr_[:, 0:128].rearrange("p (h d) -> p h d", h=2), krn,
                                      B["s8"][0], B["r8"][0], B["s8"][1], B["r8"][1])
            yield
            yield
            P.op("pe", lambda e: e.transpose(out=psT[:, 0:128], in_=kr_[:, 0:128], identity=idb[:]),
                 reads=[krn, "idb"], writes=["psT"])
            P.op("dve", lambda e: e.tensor_copy(out=kst_[:], in_=psT[:, 0:128]), reads=["psT"], writes=[kstn])
            P.dma(store_eng, lambda e: e.dma_start(out=kdst[:, t * 128:(t + 1) * 128], in_=kst_[:]), kstn,
                  reads=[kstn], writes=[kdname + B["tag"]])
            yield

        BS_PRO = make_bufset("Q", xq, rpq, hT3, st1c, rs1c, kf, ka, kr, st8p, rs8p, kst, vst)
        BS_S1 = make_bufset("S", xs1, rp1, hT1, st1a, rs1a, qf, qa, qr, st8, rs8, kst2, vst2)
        P.alias.update({"xS": "xs1", "ropeS": "rp1", "hTS": "hT1", "s1S": "st1a", "r1S": "rs1a",
                        "kfS": "scr1", "kaS": "scr1", "krS": "qr", "s8S": "st8", "r8S": "rs8"})

        BS_P3 = make_bufset("P", xp, th[:, 0:128], hT2, st1, rs1, scr[:, 0:512], scr[:, 512:1024], mixT, den, rden,
                            diffT[:, 0:128], diffT[:, 128:256])
        P.alias.update({"xP": "xp", "ropeP": "th", "hTP": "hT2", "s1P": "st1", "r1P": "rs1", "kfP": "scr", "kaP": "scr",
                        "krP": "mixT", "s8P": "den", "r8P": "rden", "kstP": "diffT", "vstP": "diffT"})

        def gen_PRO(J, t):
            load_x(xq, "xQ", J["xs"], t, J["xsname"])
            load_rope(rpq, "ropeQ", t, J["rope"])
            yield
            yield from gen_prologue(J["kdst"], J["kdname"], J["vdst"], J["vdname"], t, BS_PRO)

        Wkv_full = WinE

        def gen_S1(J, t, src2d, sname, gi, nxt):
            NTq = J["ntq"]
            yield from gen_rmsnorm(xs1, "xs1", hT1, "hT1", hT1, "hT1", st1a, "st1a", rs1a, "rs1a")
            if nxt is not None:
                load_x(xs1, "xs1", nxt[2], nxt[1], nxt[3])
            yield
            pm, pname = proj(WinE, "WinE", 0, 512, hT1, "hT1")
            P.op("dve", lambda e, pm=pm: e.tensor_copy(out=pu[gi % 4][:], in_=pm[:]), reads=[pname], writes=[f"pu{gi % 4}"])
            yield
            pm, pname = proj(WinE, "WinE", 1024, 512, hT1, "hT1")
            P.op("dve", lambda e, pm=pm: e.tensor_copy(out=qf, in_=pm[:]), reads=[pname], writes=["qf"])
            yield
            qgen = head_norm_rope((qf, qa), ("qf", "qa"), 8, gq, "gq", rp1, "rp1",
                                  qr[:].rearrange("p (j g d) -> p g j d", j=4, g=2), "qr", st8, rs8, "st8", "rs8")
            next(qgen)
            yield
            pm, pname = proj(WinE, "WinE", 512, 512, hT1, "hT1")
            silu2(pm, pname, spz[gi % 3][:], f"spz{gi % 3}")
            yield
            next(qgen)
            yield
            pm, pname = proj(WinE, "WinE", 1792, 512, hT1, "hT1")
            silu2(pm, pname, saz[gi % 3][:], f"saz{gi % 3}")
            yield
            for _ in qgen:
                yield
            yield
            yield
            if nxt is not None:
                load_rope(rp1, "rp1", nxt[1], nxt[0]["rope"])
            s = gi % 2
            for j in range(4):
                P.op("pe", lambda e, j=j: e.transpose(out=psT[:, j * 128:(j + 1) * 128], in_=qr[:, j * 128:(j + 1) * 128],
                                                      identity=idb[:]),
                     reads=["qr", "idb"], writes=["psT"])
            P.op("dve", lambda e: e.tensor_copy(out=QTz[s][0][0:64, :], in_=psT[0:64, 0:512]), reads=["psT"],
                 writes=[f"QTz{s}"])
            P.op("dve", lambda e: e.tensor_copy(out=QTz[s][1][64:128, :], in_=psT[64:128, 0:512]), reads=["psT"],
                 writes=[f"QTz{s}"])
            yield

        def gen_ATT(J, t, gi):
            s = gi % 2
            NI = 2 * NT
            LAG = 2

            def qk(i):
                c, g = i // 2, i % 2
                b = i % 3
                P.op("pe", lambda e: e.matmul(psS[b][:], lhsT=KT[:, c * 128:(c + 1) * 128], rhs=QTz[s][g][:],
                                              start=True, stop=True),
                     reads=["KT", f"QTz{s}"], writes=[f"psS{b}"])
                P.op("act", lambda e: e.activation(out=pT[b][:], in_=psS[b][:], func=AF.Exp, scale=0.125),
                     reads=[f"psS{b}"], writes=[f"pT{b}"])

            def pv(i):
                c, g = i // 2, i % 2
                b = i % 3
                P.op("pe", lambda e: e.matmul(psO[g][0:VW, :], lhsT=V[:, c, g, :], rhs=pT[b][:],
                                              start=(c == 0), stop=(c == NT - 1)),
                     reads=["V", f"pT{b}"], writes=[f"psO{g}"])

            for i in range(NI + LAG):
                if i < NI:
                    qk(i)
                if i >= LAG:
                    pv(i - LAG)
                yield

        def gen_POST(J, t, src2d, sname, dst2d, dname_, gi):
            NTq = J["ntq"]
            bc_t, bc_n = J["bandc"], J["bcname"]
            load_x(xp, "xp", src2d, t, sname)
            for g in range(2):
                P.op("dve", lambda e, g=g: e.tensor_copy(out=oT[g][0:VW, :], in_=psO[g][0:VW, :]),
                     reads=[f"psO{g}"], writes=[f"oT{g}"])
            yield
            pm, pname = next_mm()
            var = 0 if t == 0 else (2 if t == NTq - 1 else 1)
            for g in range(4):
                hp = t > 0
                hn = t < NTq - 1
                P.op("pe", lambda e, g=g, pm=pm: e.matmul(pm[:, g * 128:(g + 1) * 128], lhsT=pu[gi % 4][:, g * 128:(g + 1) * 128],
                                                          rhs=bc_t[:, g * 3 + var, :], start=True, stop=not (hp or hn)),
                     reads=[f"pu{gi % 4}", bc_n], writes=[pname])
                if hp:
                    P.op("pe", lambda e, g=g, pm=pm: e.matmul(pm[:, g * 128:g * 128 + 8],
                                                              lhsT=pu[(gi - 1) % 4][:, g * 128:(g + 1) * 128],
                                                              rhs=bandp[:, g, :], start=False, stop=not hn),
                         reads=[f"pu{(gi - 1) % 4}", "bandp"], writes=[pname])
                if hn:
                    P.op("pe", lambda e, g=g, pm=pm: e.matmul(pm[:, g * 128 + 120:g * 128 + 128],
                                                              lhsT=pu[(gi + 1) % 4][:, g * 128:(g + 1) * 128],
                                                              rhs=bandn[:, g, :], start=False, stop=True),
                         reads=[f"pu{(gi + 1) % 4}", "bandn"], writes=[pname])
            P.op("dve", lambda e, pm=pm: e.tensor_copy(out=diffT[:], in_=pm[:]), reads=[pname], writes=["diffT"])
            yield
            pm, pname = next_mm()
            for g in range(4):
                P.op("pe", lambda e, g=g, pm=pm: e.matmul(pm[:, g * 128:(g + 1) * 128], lhsT=diffT[:, g * 128:(g + 1) * 128],
                                                          rhs=poolW[:, g, :], start=True, stop=True),
                     reads=["diffT", "poolW"], writes=[pname])
            P.op("dve", lambda e, pm=pm: e.tensor_tensor(out=mix[:, 0:512], in0=pm[:], in1=spz[gi % 3][:], op=ALU.mult),
                 reads=[pname, f"spz{gi % 3}"], writes=["mix"])
            yield
            for g in range(2):
                pm, pname = next_mm()
                for j in range(4):
                    P.op("pe", lambda e, g=g, j=j, pm=pm: e.transpose(out=pm[:, j * VW:(j + 1) * VW],
                                                                      in_=oT[g][0:VW, j * 128:(j + 1) * 128],
                                                                      identity=idf[0:VW, 0:VW]),
                         reads=[f"oT{g}", "idf"], writes=[pname])
                dview = pm[:, 0:4 * VW].rearrange("p (j w) -> p j w", w=VW)[:, :, 64]
                P.op("dve", lambda e, g=g, dview=dview: e.tensor_scalar(out=den[:, g * 4:(g + 1) * 4], in0=dview, scalar1=2.0,
                                                                        scalar2=None, op0=ALU.mult),
                     reads=[pname], writes=["den"])
                P.op("dve", lambda e, g=g: e.reciprocal(out=rden[:, g * 4:(g + 1) * 4], in_=den[:, g * 4:(g + 1) * 4]),
                     reads=["den"], writes=["rden"])
                for j in range(4):
                    hh = 4 * g + j
                    P.op("dve", lambda e, pm=pm, j=j, hh=hh: e.scalar_tensor_tensor(
                        out=mix[:, 512 + hh * 64:512 + (hh + 1) * 64], in0=pm[:, j * VW:j * VW + 64],
                        scalar=rden[:, hh:hh + 1], in1=saz[gi % 3][:, hh * 64:(hh + 1) * 64], op0=ALU.mult, op1=ALU.mult),
                        reads=[pname, "rden", f"saz{gi % 3}"], writes=["mix"])
                yield
            yield
            yield
            yield
            for k in range(8):
                P.op("pe", lambda e, k=k: e.transpose(out=psT[:, k * 128:(k + 1) * 128], in_=mix[:, k * 128:(k + 1) * 128],
                                                      identity=idb[:]),
                     reads=["mix", "idb"], writes=["psT"])
            P.op("dve", lambda e: e.tensor_copy(out=mixT[:], in_=psT[:]), reads=["psT"], writes=["mixT"])
            yield
            yield
            for sl in range(2):
                pm, pname = proj(WoutE, "WoutE", sl * 512, 512, mixT, "mixT")
                P.op("dve", lambda e, pm=pm, sl=sl: e.tensor_tensor(out=xp[:, sl * 512:(sl + 1) * 512], in0=pm[:],
                                                                    in1=xp[:, sl * 512:(sl + 1) * 512], op=ALU.add),
                     reads=[pname, "xp"], writes=["xp"])
                yield
            yield
            yield from gen_rmsnorm(xp, "xp", hT2, "hT2", hT2, "hT2", st1, "st1", rs1, "rs1", ny=3)
            yield
            for sl in range(2):
                pm, pname = proj(WinO, "WinO", sl * 512, 512, hT2, "hT2")
                P.op("dve", lambda e, pm=pm, sl=sl: e.tensor_copy(out=u[:, sl * 512:(sl + 1) * 512], in_=pm[:]),
                     reads=[pname], writes=["u"])
                yield
            for sl in range(2):
                pm, pname = proj(WinO, "WinO", 1024 + sl * 512, 512, hT2, "hT2")
                P.op("dve", lambda e, pm=pm, sl=sl: e.tensor_copy(out=vv[:, sl * 512:(sl + 1) * 512], in_=pm[:]),
                     reads=[pname], writes=["vv"])
                yield
            P.op("act", lambda e: e.activation(out=vv, in_=vv, func=AF.Gelu), reads=["vv"], writes=["vv"])
            P.op("act", lambda e: e.activation(out=u, in_=u, func=AF.Gelu), reads=["u"], writes=["u"])
            for sl in range(2):
                pm, pname = proj(WinO, "WinO", 2048 + sl * 512, 512, hT2, "hT2")
                silu2(pm, pname, tmpf[:], "tmpf")
                P.op("pool", lambda e, sl=sl: e.tensor_tensor(out=u[:, sl * 512:(sl + 1) * 512], in0=u[:, sl * 512:(sl + 1) * 512],
                                                              in1=tmpf[:], op=ALU.mult),
                     reads=["u", "tmpf"], writes=["u"])
                yield
            P.op("dve", lambda e: e.scalar_tensor_tensor(out=vvn[:], in0=vv, scalar=1.0, in1=vv, op0=ALU.mult,
                                                         op1=ALU.mult, accum_out=st1[:]),
                 reads=["vv"], writes=["vvn", "st1"])
            yield
            P.op("dve", lambda e: e.tensor_scalar(out=rs1[:], in0=st1[:], scalar1=1.0 / D, scalar2=EPS, op0=ALU.mult,
                                                  op1=ALU.add),
                 reads=["st1"], writes=["rs1"])
            P.op("pool", lambda e: e.tensor_tensor(out=rs1[:], in0=rs1[:], in1=mhalf[:, 0:1], op=ALU.pow),
                 reads=["rs1", "mhalf"], writes=["rs1"])
            yield
            P.op("dve", lambda e: e.scalar_tensor_tensor(out=vvn[:], in0=vv, scalar=rs1[:], in1=gS[:], op0=ALU.mult,
                                                         op1=ALU.mult),
                 reads=["vv", "rs1", "gS"], writes=["vvn"])
            yield
            yield
            for half in range(2):
                pm, pname = next_mm()
                for j in range(4):
                    hh = half * 4 + j
                    P.op("pe", lambda e, pm=pm, j=j, hh=hh: e.matmul(pm[:, j * 128:(j + 1) * 128], lhsT=WsT[:, hh, :],
                                                                    rhs=vvn[:, hh * 128:(hh + 1) * 128], start=True, stop=True),
                         reads=["WsT", "vvn"], writes=[pname])
                for j in range(4):
                    hh = half * 4 + j
                    P.op("dve", lambda e, pm=pm, j=j, hh=hh: e.scalar_tensor_tensor(
                        out=mix[:, hh * 128:(hh + 1) * 128], in0=pm[:, j * 128:(j + 1) * 128], scalar=bsT[:, hh:hh + 1],
                        in1=u[:, hh * 128:(hh + 1) * 128], op0=ALU.add, op1=ALU.mult),
                        reads=[pname, "bsT", "u"], writes=["mix"])
                yield
            yield
            yield
            for k in range(8):
                P.op("pe", lambda e, k=k: e.transpose(out=psT[:, k * 128:(k + 1) * 128], in_=mix[:, k * 128:(k + 1) * 128],
                                                      identity=idb[:]),
                     reads=["mix", "idb", "vvn"], writes=["psT"])
            P.op("dve", lambda e: e.tensor_copy(out=mixT[:], in_=psT[:]), reads=["psT"], writes=["mixT"])
            yield
            yield
            for sl in range(2):
                pm, pname = proj(WoutO, "WoutO", sl * 512, 512, mixT, "mixT")
                P.op("dve", lambda e, pm=pm, sl=sl: e.scalar_tensor_tensor(
                    out=xp[:, sl * 512:(sl + 1) * 512], in0=pm[:], scalar=0.5, in1=xp[:, sl * 512:(sl + 1) * 512],
                    op0=ALU.mult, op1=ALU.add),
                    reads=[pname, "xp"], writes=["xp"])
                yield
            P.dma("sp", lambda e: e.dma_start(out=dst2d[t * 128:(t + 1) * 128, :], in_=xp[:]), "stx",
                  reads=["xp"], writes=[dname_])

        def load_kv(J, layer_pair):
            P.op("pool", lambda e: e.memset(V[:], 1.0), writes=["V"])
            if J["half"] and layer_pair > 0:
                kg, vg = kvg_k.ap(), kvg_v.ap()
                o2 = (NTL - HT) * 128
                P.dma("sp", lambda e: [e.dma_start(out=KT[:, 0:HT * 128], in_=kg[0:128, 0:HT * 128]),
                                       e.dma_start(out=KT[:, HT * 128:L], in_=kg[128:256, o2:NTL * 128])],
                      "KT", reads=["kvg_k"], writes=["KT"], n=2)

                def vfn(e):
                    res = []
                    for g in range(2):
                        res.append(e.dma_start(out=V[:, 0:HT, g, 0:64],
                                               in_=vg[0:HT * 128, g * 64:(g + 1) * 64].rearrange("(c p) d -> p c d", p=128)))
                        res.append(e.dma_start(out=V[:, HT:NT, g, 0:64],
                                               in_=vg[NTL * 128 + o2:2 * NTL * 128, g * 64:(g + 1) * 64]
                                               .rearrange("(c p) d -> p c d", p=128)))
                    return res
                P.dma("sp", vfn, "V", reads=["kvg_v"], writes=["V"], n=4)
            else:
                ksrc = kts_d[J["pa"]]
                vsrc = vs_d[J["pa"]]
                P.dma("sp", lambda e: e.dma_start(out=KT[:], in_=ksrc), "KT", reads=["kts_d" + x_ for x_ in "QSP"],
                      writes=["KT"])
                P.dma("sp", lambda e: [e.dma_start(out=V[:, :, g, 0:64],
                                                   in_=vsrc[:, g * 64:(g + 1) * 64].rearrange("(c p) d -> p c d", p=128))
                                       for g in range(2)],
                      "V", reads=["vs_d" + x_ for x_ in "QSP"], writes=["V"], n=2)

        def count_steps(genfn):
            saved = (P.op, P.dma, mmi[0])
            P.op = lambda *a, **k: None
            P.dma = lambda *a, **k: None
            n = sum(1 for _ in genfn())
            P.op, P.dma = saved[0], saved[1]
            mmi[0] = saved[2]
            return n

        FRAC = 94

        def run_slot(gens):
            gens = [(g, n) for g, n in gens if g is not None]
            if not gens:
                return
            main, nmain = gens[0]
            others = gens[1:]
            done = [0] * len(others)
            for oi, (g, n) in enumerate(others):
                if next(g, "END") != "END":
                    done[oi] += 1
            for i in range(nmain):
                next(main, None)
                for oi, (g, n) in enumerate(others):
                    target = ((i + 1) * n * 100 + nmain * FRAC - 1) // (nmain * FRAC)
                    while done[oi] < min(target, n):
                        next(g, None)
                        done[oi] += 1
            for oi, (g, n) in enumerate(others):
                for _ in g:
                    pass
            for _ in main:
                pass

        npairs = depth // 2
        issue_pair_dmas(0)
        post_winE()
        bcastload(gk, "gk", k_norm[0:1, :])
        def gen_passA(seq, tiles, B):
            for t in tiles:
                load_x(B["x"][0], B["x"][1], x_d[seq], t)
                load_rope(B["rope"][0], B["rope"][1], t)
                yield from gen_prologue(kts_d[seq], "kts_d", vs_d[seq], "vs_d", t, B, store_eng="act")

        for seq in range(NSEQ):
            ga = gen_passA(seq, range(0, NT, 3), BS_PRO)
            gb = gen_passA(seq, range(1, NT, 3), BS_S1)
            gc_ = gen_passA(seq, range(2, NT, 3), BS_P3)
            for _ in range(8):
                next(ga, None)
            for _ in range(4):
                next(gb, None)
            alive = True
            while alive:
                alive = False
                for g in (ga, gb, gc_):
                    if next(g, "END") != "END":
                        alive = True
        post_rest()
        for j in range(npairs):
            last = (j == npairs - 1)
            if j > 0:
                issue_pair_dmas(j)
                post_winE()
                post_rest()
            if not last:
                load_kv_weights(j + 1)
            items = []
            for J in JOBS:
                src2d, sname = (J["xin"], None) if j == 0 else (J["xs"], J["xsname"])
                dst2d, dname_ = (J["out"], "out_" + J["name"]) if last else (J["xs"], J["xsname"])
                for t in range(J["ntq"]):
                    items.append((J, t, src2d, sname, dst2d, dname_))
            NI_ = len(items)
            I0 = items[0]
            load_x(xs1, "xs1", I0[2], I0[1], I0[3])
            load_rope(rp1, "rp1", I0[1], I0[0]["rope"])
            nS1 = count_steps(lambda: gen_S1(I0[0], 1, I0[2], I0[3], 1, None))
            nATT = count_steps(lambda: gen_ATT(I0[0], 1, 1))
            nPOST = count_steps(lambda: gen_POST(I0[0], 1, I0[2], I0[3], I0[4], I0[5], 1))
            nPRO = count_steps(lambda: gen_PRO(I0[0], 1))
            for k in range(-1, NI_ + 2):
                att = post = s1 = pro = (None, 0)
                if 0 <= k < NI_:
                    J, t = items[k][0], items[k][1]
                    if t == 0:
                        load_kv(J, j)
                    att = (gen_ATT(J, t, k), nATT)
                if 0 <= k - 1 < NI_:
                    J, t, a, b, c_, d_ = items[k - 1]
                    post = (gen_POST(J, t, a, b, c_, d_, k - 1), nPOST)
                if 0 <= k + 1 < NI_:
                    J, t, a, b, c_, d_ = items[k + 1]
                    nx = items[k + 2] if k + 2 < NI_ else None
                    s1 = (gen_S1(J, t, a, b, k + 1, None if nx is None else (nx[0], nx[1], nx[2], nx[3])), nS1)
                if (not last) and 0 <= k - 2 < NI_:
                    pro = (gen_PRO(items[k - 2][0], items[k - 2][1]), nPRO)
                if att[0] is not None:
                    run_slot([att, post, s1, pro])
                else:
                    run_slot([post, s1, pro])
            if half and not last:
                groups = [[c, c + 4] for c in range(4)]
                P.cc(lambda e: e.collective_compute("AllGather", ALU.bypass, replica_groups=groups,
                                                    ins=[kvl_k.ap().opt()], outs=[kvg_k.ap().opt()]),
                     "cck", reads=["kvl_k" + x_ for x_ in "QSP"], writes=["kvg_k"])
                P.cc(lambda e: e.collective_compute("AllGather", ALU.bypass, replica_groups=groups,
                                                    ins=[kvl_v.ap().opt()], outs=[kvg_v.ap().opt()]),
                     "ccv", reads=["kvl_v" + x_ for x_ in "QSP"], writes=["kvg_v"])
        P.emit()
    return nc


_CACHE = {}


_BANDS = band_tables()


def _consts(L):
    return {
        "ident_b": np.eye(128, dtype=np.float32).astype(ml_dtypes.bfloat16),
        "ident_f": np.eye(128, dtype=np.float32),
        "rope": rope_table(L),
        "band_c": _BANDS[0], "band_p": _BANDS[1], "band_n": _BANDS[2],
    }


def run_step(x_prompt, x_sample, weights, depth=4):
    L = x_prompt.shape[1]
    NT = L // 128
    NTL = half_tiles(NT)
    HT = NT // 2
    key = (L, depth)
    if key not in _CACHE:
        _CACHE[key] = build_program(L, 2, depth, half=True)
    nc = _CACHE[key]
    consts = _consts(L)
    bc = consts["band_c"]
    in_maps = []
    for c in range(N_CORES):
        lower = c < 4
        g0 = 0 if lower else NT - NTL
        xs_ = x_sample[c % 4]
        m = {"x": np.ascontiguousarray(np.stack([x_prompt[c], xs_], axis=0), dtype=np.float32),
             "xh": np.ascontiguousarray(xs_[g0 * 128:(g0 + NTL) * 128], dtype=np.float32),
             "rope_h": np.ascontiguousarray(consts["rope"][g0 * 128:(g0 + NTL) * 128])}
        sel = [0, 1, 1] if lower else [1, 1, 2]
        m["band_ch"] = np.ascontiguousarray(bc[:, [g * 3 + v for g in range(4) for v in sel], :])
        for k, v in weights.items():
            m[k] = np.ascontiguousarray(v, dtype=np.float32)
        m.update(consts)
        in_maps.append(m)
    res = run_bass_kernel_spmd(nc, in_maps, core_ids=list(range(N_CORES)))
    outs = res.results
    y_prompt = np.stack([np.asarray(outs[c]["out"]) for c in range(N_CORES)], axis=0).astype(np.float32)
    y_sample = np.stack([np.concatenate([np.asarray(outs[s_]["out_h"])[0:HT * 128],
                                         np.asarray(outs[s_ + 4]["out_h"])[(NTL - HT) * 128:NTL * 128]], axis=0)
                         for s_ in range(4)], axis=0).astype(np.float32)
    return y_prompt, y_sample


def kernel(x_prompt, x_sample, norm_e, w_in_e, pool_w, pool_scale, q_norm, k_norm, w_out_e,
           norm_o, w_in_o, sgu_norm, w_s, b_s, w_out_o):
    x_prompt = np.asarray(x_prompt)
    x_sample = np.asarray(x_sample)
    weights = dict(norm_e=norm_e, w_in_e=w_in_e, pool_w=pool_w, pool_scale=pool_scale, q_norm=q_norm,
                   k_norm=k_norm, w_out_e=w_out_e, norm_o=norm_o, w_in_o=w_in_o, sgu_norm=sgu_norm,
                   w_s=w_s, b_s=b_s, w_out_o=w_out_o)
    weights = {k: np.asarray(v) for k, v in weights.items()}
    return run_step(x_prompt, x_sample, weights)
```

```python
import contextlib
import numpy as np
import ml_dtypes
import concourse.bass as bass
import concourse.mybir as mybir
from concourse.bass_utils import run_bass_kernel_spmd

F32 = mybir.dt.float32
BF16 = mybir.dt.bfloat16
AF = mybir.ActivationFunctionType
ALU = mybir.AluOpType
AX = mybir.AxisListType

ENGS = ("pe", "act", "dve", "pool", "sp")

D = 1024
EPS = 1e-6
GRID_W = 64
POOL_WINDOWS = (2, 4, 8, 16)
VW = 66
N_CORES = 8


class Prog:
    def __init__(self, nc):
        self.nc = nc
        self.streams = {e: [] for e in ENGS}
        self.count = {e: 0 for e in ENGS}
        self.seen = {e: {} for e in ENGS}
        self.last_w = {}
        self.readers = {}
        self.dma_count = {}
        self.sem_keys = []
        self.alias = {}

    def _semkey(self, k):
        if k not in self.sem_keys:
            self.sem_keys.append(k)
        return k

    def _deps(self, eng, reads, writes):
        evs = {}
        reads = [self.alias.get(r, r) for r in reads]
        writes = [self.alias.get(w, w) for w in writes]

        def add(ev):
            if ev is None:
                return
            k, v, e = ev
            if e == "pe" and eng == "pe":
                return
            if v > evs.get(k, 0):
                evs[k] = v

        for r in reads:
            add(self.last_w.get(r))
        for w in writes:
            add(self.last_w.get(w))
            for k, (v, e) in self.readers.get(w, {}).items():
                add((k, v, e))
        out = []
        seen = self.seen[eng]
        for k, v in evs.items():
            if seen.get(k, 0) >= v:
                continue
            seen[k] = v
            out.append((k, v))
        return out

    def _record(self, ev, reads, writes):
        k, v, e = ev
        reads = [self.alias.get(r, r) for r in reads]
        writes = [self.alias.get(w, w) for w in writes]
        for r in reads:
            self.readers.setdefault(r, {})[k] = (v, e)
        for w in writes:
            self.last_w[w] = ev
            self.readers[w] = {}

    def op(self, eng, fn, reads=(), writes=()):
        waits = self._deps(eng, reads, writes)
        self.count[eng] += 1
        k = self._semkey("e:" + eng)
        ev = (k, self.count[eng], eng)
        self.streams[eng].append((waits, fn, (k, 1)))
        self._record(ev, reads, writes)
        return ev

    def dma(self, eng, fn, key, reads=(), writes=(), n=1):
        waits = self._deps(eng, reads, writes)
        k = self._semkey("d:" + key)
        self.dma_count[k] = self.dma_count.get(k, 0) + 16 * n
        ev = (k, self.dma_count[k], "dma")
        self.streams[eng].append((waits, fn, (k, 16)))
        self._record(ev, reads, writes)
        return ev

    def cc(self, fn, key, reads=(), writes=()):
        waits = self._deps("pool", reads, writes)
        k = self._semkey("c:" + key)
        self.dma_count[k] = self.dma_count.get(k, 0) + 1
        ev = (k, self.dma_count[k], "dma")
        self.streams["pool"].append((waits, fn, (k, 1)))
        self._record(ev, reads, writes)
        return ev

    def emit(self, final_wait_eng="sp"):
        nc = self.nc
        with contextlib.ExitStack() as st:
            sems = {}
            for k in self.sem_keys:
                sems[k] = st.enter_context(nc.semaphore(k.replace(":", "_")))
            block = st.enter_context(nc.Block())
            finals = list(self.dma_count.items())

            def run(engine, name):
                for waits, fn, (k, inc) in self.streams[name]:
                    for wk, wv in waits:
                        engine.wait_ge(sems[wk], wv)
                    res = fn(engine)
                    if not isinstance(res, (list, tuple)):
                        res = [res]
                    for ins in res:
                        ins.then_inc(sems[k], inc)
                if name == final_wait_eng:
                    for fk, fv in finals:
                        engine.wait_ge(sems[fk], fv)
                    for e2 in ("pe", "act", "dve", "pool"):
                        kk = "e:" + e2
                        if kk in sems and self.count[e2] > 0:
                            engine.wait_ge(sems[kk], self.count[e2])

            @block.tensor
            def _(e):
                run(e, "pe")

            @block.scalar
            def _(e):
                run(e, "act")

            @block.vector
            def _(e):
                run(e, "dve")

            @block.gpsimd
            def _(e):
                run(e, "pool")

            @block.sync
            def _(e):
                run(e, "sp")


def rope_table(L):
    t = np.arange(L)
    row = (t // GRID_W).astype(np.float32)
    col = (t % GRID_W).astype(np.float32)
    inv = (1.0 / (np.float32(10000.0) ** (np.arange(16, dtype=np.float32) / np.float32(16)))).astype(np.float32)
    ar = (row[:, None] * inv[None, :]).astype(np.float32)
    ac = (col[:, None] * inv[None, :]).astype(np.float32)
    cr, sr, cc, sc = np.cos(ar), np.sin(ar), np.cos(ac), np.sin(ac)
    C = np.concatenate([cr, cr, cc, cc], axis=1)
    S = np.concatenate([-sr, sr, -sc, sc], axis=1)
    return np.ascontiguousarray(np.concatenate([C, S], axis=1).astype(np.float32))


def band_tables():
    bc = np.zeros((128, 12, 128), np.float32)
    bp = np.zeros((128, 4, 8), np.float32)
    bn = np.zeros((128, 4, 8), np.float32)
    for g, w in enumerate(POOL_WINDOWS):
        hw = w // 2
        for var in range(3):
            for t in range(128):
                if var == 0:
                    lo, hi = max(t - hw, 0), t + hw
                elif var == 1:
                    lo, hi = t - hw, t + hw
                else:
                    lo, hi = t - hw, min(t + hw, 128)
                cnt = hi - lo
                for s_ in range(max(lo, 0), min(hi, 128)):
                    bc[s_, g * 3 + var, t] += 1.0 / cnt
                bc[t, g * 3 + var, t] -= 1.0
        for t in range(128):
            lo, hi = t - hw, t + hw
            for s_ in range(lo, 0):
                bp[128 + s_, g, t] += 1.0 / w
            for s_ in range(128, hi):
                bn[s_ - 128, g, t - 120] += 1.0 / w
    bf = ml_dtypes.bfloat16
    return bc.astype(bf), bp.astype(bf), bn.astype(bf)


def half_tiles(NT):
    return NT // 2 + 2


def build_program(L, NSEQ, depth=4, half=True):
    NT = L // 128
    assert NT >= 4 and NSEQ == 2
    NTL = half_tiles(NT) if half else NT
    HT = NT // 2
    nc = bass.Bass("TRN2", target_bir_lowering=False)

    def din(name, shape, dt=F32):
        return nc.dram_tensor(name, list(shape), dt, kind="ExternalInput").ap()

    x_d = din("x", [NSEQ, L, D])
    norm_e = din("norm_e", [2, D])
    w_in_e = din("w_in_e", [2, D, 2304])
    pool_w = din("pool_w", [2, 4, 128, 128])
    pool_scale = din("pool_scale", [2, 512])
    q_norm = din("q_norm", [2, 64])
    k_norm = din("k_norm", [2, 64])
    w_out_e = din("w_out_e", [2, D, D])
    norm_o = din("norm_o", [2, D])
    w_in_o = din("w_in_o", [2, D, 3072])
    sgu_norm = din("sgu_norm", [2, D])
    w_s = din("w_s", [2, 8, 128, 128])
    b_s = din("b_s", [2, 8, 128])
    w_out_o = din("w_out_o", [2, D, D])
    ident_b = din("ident_b", [128, 128], BF16)
    ident_f = din("ident_f", [128, 128], F32)
    rope_d = din("rope", [L, 128], F32)
    bc_d = din("band_c", [128, 12, 128], BF16)
    bp_d = din("band_p", [128, 4, 8], BF16)
    bn_d = din("band_n", [128, 4, 8], BF16)
    xh_d = din("xh", [NTL * 128, D])
    ropeh_d = din("rope_h", [NTL * 128, 128], F32)
    bch_d = din("band_ch", [128, 12, 128], BF16)

    out_d = nc.dram_tensor("out", [L, D], F32, kind="ExternalOutput").ap()
    outh_d = nc.dram_tensor("out_h", [NTL * 128, D], F32, kind="ExternalOutput").ap()
    xs_d = nc.dram_tensor("xs", [L, D], F32, kind="Internal").ap()
    xsh_d = nc.dram_tensor("xsh", [NTL * 128, D], F32, kind="Internal").ap()
    kvl_k = nc.dram_tensor("kvl_k", [128, NTL * 128], BF16, kind="Internal")
    kvl_v = nc.dram_tensor("kvl_v", [NTL * 128, 128], BF16, kind="Internal")
    kvg_k = nc.dram_tensor("kvg_k", [2 * 128, NTL * 128], BF16, kind="Internal")
    kvg_v = nc.dram_tensor("kvg_v", [2 * NTL * 128, 128], BF16, kind="Internal")
    kts_d = nc.dram_tensor("kts", [NSEQ, 128, L], BF16, kind="Internal").ap()
    vs_d = nc.dram_tensor("vs", [NSEQ, L, 128], BF16, kind="Internal").ap()

    with contextlib.ExitStack() as st:
        def sb(name, shape, dt):
            return st.enter_context(nc.sbuf_tensor("s_" + name, list(shape), dt))

        def ps(name, shape, dt):
            return st.enter_context(nc.psum_tensor("p_" + name, list(shape), dt))

        WinE = sb("WinE", [128, 8, 2304], BF16)
        WoutE = sb("WoutE", [128, 8, 1024], BF16)
        poolW = sb("poolW", [128, 4, 128], BF16)
        WinO = sb("WinO", [128, 8, 3072], BF16)
        WoutO = sb("WoutO", [128, 8, 1024], BF16)
        WsT = sb("WsT", [128, 8, 128], BF16)
        gS = sb("gS", [128, D], F32)
        gcE = sb("gcE", [128, 8], F32)
        gcO = sb("gcO", [128, 8], F32)
        gcEn = sb("gcEn", [128, 8], F32)
        gq = sb("gq", [128, 64], F32)
        gk = sb("gk", [128, 64], F32)
        bsT = sb("bsT", [128, 8], F32)
        idb = sb("idb", [128, 128], BF16)
        idf = sb("idf", [128, VW], F32)
        bandc = sb("bandc", [128, 12, 128], BF16)
        bandch = sb("bandch", [128, 12, 128], BF16)
        bandp = sb("bandp", [128, 4, 8], BF16)
        bandn = sb("bandn", [128, 4, 8], BF16)
        mhalf = sb("mhalf", [128, 8], F32)
        KT = sb("KT", [128, L], BF16)
        V = sb("V", [128, NT, 2, VW], BF16)
        xs1 = sb("xs1", [128, D], F32)
        rp1 = sb("rp1", [128, 128], F32)
        hT1 = sb("hT1", [128, D], BF16)
        scr1 = sb("scr1", [128, D], F32)
        qf = scr1[:, 0:512]
        qa = scr1[:, 512:1024]
        qr = sb("qr", [128, 512], BF16)
        st8 = sb("st8", [128, 8], F32)
        rs8 = sb("rs8", [128, 8], F32)
        pu = [sb(f"pu{i}", [128, 512], BF16) for i in range(4)]
        spz = [sb(f"spz{i}", [128, 512], BF16) for i in range(3)]
        saz = [sb(f"saz{i}", [128, 512], BF16) for i in range(3)]
        QTz = [[sb(f"QTz{i}{g}", [128, 512], BF16) for g in range(2)] for i in range(2)]
        th = sb("th", [128, 512], F32)
        tmpf = sb("tmpf", [128, 512], F32)
        st1a = sb("st1a", [128, 1], F32)
        rs1a = sb("rs1a", [128, 1], F32)
        st1 = sb("st1", [128, 1], F32)
        rs1 = sb("rs1", [128, 1], F32)
        pT = [sb(f"pT{i}", [128, 512], BF16) for i in range(3)]
        xp = sb("xp", [128, D], F32)
        hT2 = sb("hT2", [128, D], BF16)
        scr = sb("scr", [128, D], F32)
        scr2 = sb("scr2", [128, D], F32)
        vv = scr[:, :]
        u = scr2[:, :]
        oT = [scr2[:, i * 512:(i + 1) * 512] for i in range(2)]
        den = sb("den", [128, 8], F32)
        rden = sb("rden", [128, 8], F32)
        diffT = sb("diffT", [128, 512], BF16)
        mix = sb("mix", [128, D], BF16)
        mixT = sb("mixT", [128, D], BF16)
        vvn = mixT
        kf = sb("kf", [128, 128], F32)
        ka = sb("ka", [128, 128], F32)
        kr = sb("kr", [128, 128], BF16)
        st8p = sb("st8p", [128, 8], F32)
        rs8p = sb("rs8p", [128, 8], F32)
        kst = sb("kst", [128, 128], BF16)
        vst = sb("vst", [128, 128], BF16)
        kst2 = sb("kst2", [128, 128], BF16)
        vst2 = sb("vst2", [128, 128], BF16)
        xq = sb("xq", [128, D], F32)
        rpq = sb("rpq", [128, 128], F32)
        hT3 = sb("hT3", [128, D], BF16)
        st1c = sb("st1c", [128, 1], F32)
        rs1c = sb("rs1c", [128, 1], F32)
        psT = ps("psT", [128, 1024], BF16)
        psMM = [ps(f"psMM{i}", [128, 512], F32) for i in range(2)]
        psS = [ps(f"psS{i}", [128, 512], F32) for i in range(3)]
        psO = [ps(f"psO{i}", [128, 512], F32) for i in range(2)]

        P = Prog(nc)
        P.alias = {"qf": "scr1", "qa": "scr1", "vv": "scr", "oT0": "scr2", "oT1": "scr2", "u": "scr2",
                   "vvn": "mixT", "wsn": "mix", "Wkv": "WinE"}
        wsn = mix[:, :].rearrange("p (h q) -> p h q", h=8)
        Wkv = WinE[:, :, 1536:1792]
        mmi = [0]

        def next_mm():
            i = mmi[0] % 2
            mmi[0] += 1
            return psMM[i], f"psMM{i}"

        def castload(dst, dst_name, src, ncols, chunk=1024):
            srcv = src.rearrange("(k p) n -> p k n", p=128)
            chunks = [(c, min(c + chunk, ncols)) for c in range(0, ncols, chunk)]

            def fn(e):
                return [e.dma_start(out=dst[:, :, a:b], in_=srcv[:, :, a:b]) for a, b in chunks]
            P.dma("pool", fn, dst_name, writes=[dst_name], n=len(chunks))

        def bcastload(dst, dst_name, src_row, eng="sp"):
            P.dma(eng, lambda e: e.dma_start(out=dst[:], in_=src_row.partition_broadcast(128)), dst_name,
                  writes=[dst_name])

        def colload(dst, dst_name, src_vec):
            P.dma("sp", lambda e: e.dma_start(out=dst[:], in_=src_vec.rearrange("(k p) -> p k", p=128),
                                              allow_slow_non_contiguous=True), dst_name, writes=[dst_name])

        def rowscale(W, wname, gc, gname, ncols):
            for k in range(8):
                P.op("dve", lambda e, k=k: e.tensor_scalar(out=W[:, k, 0:ncols], in0=W[:, k, 0:ncols],
                                                           scalar1=gc[:, k:k + 1], scalar2=None, op0=ALU.mult),
                     reads=[wname, gname], writes=[wname])

        P.dma("sp", lambda e: e.dma_start(out=idb[:], in_=ident_b), "idb", writes=["idb"])
        P.dma("sp", lambda e: e.dma_start(out=idf[:], in_=ident_f[:, 0:VW]), "idf", writes=["idf"])
        P.dma("sp", lambda e: e.dma_start(out=bandc[:], in_=bc_d), "bandc", writes=["bandc"])
        P.dma("sp", lambda e: e.dma_start(out=bandch[:], in_=bch_d), "bandch", writes=["bandch"])
        P.dma("sp", lambda e: e.dma_start(out=bandp[:], in_=bp_d), "bandp", writes=["bandp"])
        P.dma("sp", lambda e: e.dma_start(out=bandn[:], in_=bn_d), "bandn", writes=["bandn"])
        P.op("pool", lambda e: e.memset(mhalf[:], -0.5), writes=["mhalf"])
        for i in range(2):
            for g in range(2):
                P.op("pool", lambda e, i=i, g=g: e.memset(QTz[i][g][:], 0.0), writes=[f"QTz{i}"])

        def load_kv_weights(j):
            def fn(e):
                return [e.dma_start(out=Wkv, in_=w_in_e[j][:, 1536:1792].rearrange("(k p) n -> p k n", p=128))]
            P.dma("pool", fn, "WinE", writes=["WinE"])
            colload(gcEn, "gcEn", norm_e[j])
            for k in range(8):
                P.op("dve", lambda e, k=k: e.tensor_scalar(out=WinE[:, k, 1536:1792], in0=WinE[:, k, 1536:1792],
                                                           scalar1=gcEn[:, k:k + 1], scalar2=None, op0=ALU.mult),
                     reads=["WinE", "gcEn"], writes=["WinE"])
            bcastload(gk, "gk", k_norm[j:j + 1, :])

        def issue_pair_dmas(j):
            castload(WinE, "WinE", w_in_e[j], 2304)
            colload(gcE, "gcE", norm_e[j])
            bcastload(gq, "gq", q_norm[j:j + 1, :])
            bcastload(tmpf, "tmpf", pool_scale[j:j + 1, :])
            P.dma("pool", lambda e: e.dma_start(out=poolW[:], in_=pool_w[j].rearrange("g c d -> c g d")), "poolW",
                  writes=["poolW"])
            castload(WoutE, "WoutE", w_out_e[j], 1024)
            castload(WinO, "WinO", w_in_o[j], 3072)
            colload(gcO, "gcO", norm_o[j])
            bcastload(gS, "gS", sgu_norm[j:j + 1, :])
            P.dma("sp", lambda e: e.dma_start(out=bsT[:], in_=b_s[j].rearrange("h p -> p h"),
                                              allow_slow_non_contiguous=True), "bsT", writes=["bsT"])
            P.dma("pool", lambda e: e.dma_start(out=wsn, in_=w_s[j].rearrange("h p q -> p h q")), "wsn",
                  writes=["wsn"])
            castload(WoutO, "WoutO", w_out_o[j], 1024)

        def post_winE():
            rowscale(WinE, "WinE", gcE, "gcE", 2304)

        def post_rest():
            P.op("dve", lambda e: e.tensor_scalar(out=tmpf[:], in0=tmpf[:], scalar1=0.5, scalar2=None, op0=ALU.mult),
                 reads=["tmpf"], writes=["tmpf"])
            P.op("dve", lambda e: e.tensor_tensor(out=poolW[:].rearrange("c g d -> c (g d)"),
                                                  in0=poolW[:].rearrange("c g d -> c (g d)"), in1=tmpf[:], op=ALU.mult),
                 reads=["poolW", "tmpf"], writes=["poolW"])
            rowscale(WinO, "WinO", gcO, "gcO", 3072)
            for hh in range(8):
                P.op("pe", lambda e, hh=hh: e.transpose(out=psT[:, hh * 128:(hh + 1) * 128], in_=wsn[:, hh, :],
                                                        identity=idb[:]),
                     reads=["wsn", "idb"], writes=["psT"])
            P.op("dve", lambda e: e.tensor_copy(out=WsT[:].rearrange("p h q -> p (h q)"), in_=psT[:]),
                 reads=["psT"], writes=["WsT"])

        def gen_rmsnorm(xsrc, xname, hTdst, hTname, hb, hbn, s1, s1n, r1, r1n, ny=2):
            P.op("dve", lambda e: e.scalar_tensor_tensor(out=hb[:], in0=xsrc[:], scalar=1.0, in1=xsrc[:],
                                                         op0=ALU.mult, op1=ALU.mult, accum_out=s1[:]),
                 reads=[xname], writes=[hbn, s1n])
            for _ in range(ny):
                yield
            P.op("dve", lambda e: e.tensor_scalar(out=r1[:], in0=s1[:], scalar1=1.0 / D, scalar2=EPS,
                                                  op0=ALU.mult, op1=ALU.add),
                 reads=[s1n], writes=[r1n])
            P.op("pool", lambda e: e.tensor_tensor(out=r1[:], in0=r1[:], in1=mhalf[:, 0:1], op=ALU.pow),
                 reads=[r1n, "mhalf"], writes=[r1n])
            for _ in range(ny):
                yield
            P.op("dve", lambda e: e.tensor_scalar(out=hb[:], in0=xsrc[:], scalar1=r1[:], scalar2=None, op0=ALU.mult),
                 reads=[xname, r1n], writes=[hbn])
            for _ in range(ny):
                yield
            for k in range(8):
                P.op("pe", lambda e, k=k: e.transpose(out=psT[:, k * 128:(k + 1) * 128], in_=hb[:, k * 128:(k + 1) * 128],
                                                      identity=idb[:]),
                     reads=[hbn, "idb"], writes=["psT"])
            P.op("dve", lambda e: e.tensor_copy(out=hTdst[:], in_=psT[:]), reads=["psT"], writes=[hTname])
            yield

        def proj(W, wname, c0, ncols, lhs, lname):
            pm, pname = next_mm()
            for k in range(8):
                P.op("pe", lambda e, k=k, pm=pm: e.matmul(pm[:, 0:ncols], lhsT=lhs[:, k * 128:(k + 1) * 128],
                                                          rhs=W[:, k, c0:c0 + ncols], start=(k == 0), stop=(k == 7)),
                     reads=[lname, wname], writes=[pname])
            return pm, pname

        def head_norm_rope(bufs, names, nh, gain, gname, rope_t, rname, out_view, dname, s8, r8, s8n, r8n):
            src, ta = bufs
            sname, aname = names
            n = nh * 64
            s3 = src[:, 0:n].rearrange("p (h d) -> p h d", h=nh)
            a3 = ta[:, 0:n].rearrange("p (h d) -> p h d", h=nh)
            ceng0 = "dve" if nh == 2 else "pool"
            P.op(ceng0, lambda e: e.tensor_tensor(out=ta[:, 0:n], in0=src[:, 0:n], in1=src[:, 0:n], op=ALU.mult),
                 reads=[sname], writes=[aname])
            yield
            P.op("dve", lambda e: e.tensor_reduce(out=s8[:, 0:nh], in_=a3, axis=AX.X, op=ALU.add),
                 reads=[aname], writes=[s8n])
            P.op(ceng0, lambda e: e.tensor_scalar(out=r8[:, 0:nh], in0=s8[:, 0:nh], scalar1=1.0 / 64, scalar2=EPS,
                                                  op0=ALU.mult, op1=ALU.add),
                 reads=[s8n], writes=[r8n])
            P.op("pool", lambda e: e.tensor_tensor(out=r8[:, 0:nh], in0=r8[:, 0:nh], in1=mhalf[:, 0:nh], op=ALU.pow),
                 reads=[r8n, "mhalf"], writes=[r8n])
            yield
            P.op("dve", lambda e: e.tensor_tensor(out=s3, in0=s3, in1=r8[:, 0:nh].unsqueeze(2).to_broadcast([128, nh, 64]),
                                                  op=ALU.mult),
                 reads=[sname, r8n], writes=[sname])
            yield
            ceng = "dve"
            ceng_sw = "dve" if nh == 2 else "pool"
            P.op(ceng, lambda e: e.tensor_tensor(out=s3, in0=s3, in1=gain[:].unsqueeze(1).to_broadcast([128, nh, 64]),
                                                 op=ALU.mult),
                 reads=[sname, gname], writes=[sname])
            yield
            a5 = ta[:, 0:n].rearrange("p (h a b c) -> p h a b c", h=nh, a=2, b=2)
            s5 = src[:, 0:n].rearrange("p (h a b c) -> p h a b c", h=nh, a=2, b=2)
            S5 = rope_t[:, 64:128].rearrange("p (a b c) -> p a b c", a=2, b=2)
            for blk in range(2):
                P.op(ceng_sw, lambda e, blk=blk: e.tensor_tensor(
                    out=a5[:, :, :, blk, :], in0=s5[:, :, :, 1 - blk, :],
                    in1=S5[:, :, blk, :].unsqueeze(1).to_broadcast([128, nh, 2, 16]), op=ALU.mult),
                    reads=[sname, rname], writes=[aname])
            P.op("dve", lambda e: e.tensor_tensor(out=s3, in0=s3,
                                                  in1=rope_t[:, 0:64].unsqueeze(1).to_broadcast([128, nh, 64]), op=ALU.mult),
                 reads=[sname, rname], writes=[sname])
            yield
            if nh == 8:
                fa = ta[:, 0:n].rearrange("p (g j d) -> p g j d", g=2, j=4)
                fs = src[:, 0:n].rearrange("p (g j d) -> p g j d", g=2, j=4)
            else:
                fa, fs = a3, s3
            P.op("dve", lambda e: e.tensor_tensor(out=out_view, in0=fa, in1=fs, op=ALU.add),
                 reads=[aname, sname], writes=[dname])

        def silu2(pm, pname, dst, dname, scale_t=None, scale_name=None):
            P.op("act", lambda e: e.activation(out=th[:], in_=pm[:], func=AF.Tanh, scale=0.5),
                 reads=[pname], writes=["th"])
            if scale_t is None:
                P.op("dve", lambda e: e.scalar_tensor_tensor(out=dst, in0=th[:], scalar=1.0, in1=pm[:],
                                                             op0=ALU.add, op1=ALU.mult),
                     reads=["th", pname], writes=[dname])
            else:
                P.op("dve", lambda e: e.scalar_tensor_tensor(out=tmpf[:], in0=th[:], scalar=1.0, in1=pm[:],
                                                             op0=ALU.add, op1=ALU.mult),
                     reads=["th", pname], writes=["tmpf"])
                P.op("pool", lambda e: e.tensor_tensor(out=dst, in0=tmpf[:], in1=scale_t[:], op=ALU.mult),
                     reads=["tmpf", scale_name], writes=[dname])

        def load_x(dst, dname, src2d, t, scratch_name=None):
            P.dma("sp", lambda e: e.dma_start(out=dst[:], in_=src2d[t * 128:(t + 1) * 128, :]), dname,
                  reads=[scratch_name] if scratch_name else [], writes=[dname])

        def load_rope(dst, dname, t, rope2d=None):
            r2 = rope_d if rope2d is None else rope2d
            P.dma("sp", lambda e: e.dma_start(out=dst[:], in_=r2[t * 128:(t + 1) * 128, :]), dname, writes=[dname])

        J_F = dict(name="F", ntq=NT, xin=x_d[0], rope=rope_d, xs=xs_d, xsname="xs_d", out=out_d, bandc=bandc,
                   bcname="bandc", pa=0, kdst=kts_d[0], vdst=vs_d[0], kdname="kts_d", vdname="vs_d", half=False)
        if half:
            J_H = dict(name="H", ntq=NTL, xin=xh_d, rope=ropeh_d, xs=xsh_d, xsname="xsh_d", out=outh_d, bandc=bandch,
                       bcname="bandch", pa=1, kdst=kvl_k.ap(), vdst=kvl_v.ap(), kdname="kvl_k", vdname="kvl_v", half=True)
        else:
            J_H = dict(name="H", ntq=NT, xin=x_d[1], rope=rope_d, xs=xsh_d, xsname="xsh_d", out=outh_d, bandc=bandc,
                       bcname="bandc", pa=1, kdst=kts_d[1], vdst=vs_d[1], kdname="kts_d", vdname="vs_d", half=False)
        JOBS = [J_F, J_H]

        def make_bufset(tag, x, rope, hT, s1, r1, f, a, r, s8, r8, ks, vs_):
            return dict(tag=tag, x=(x, "x" + tag), rope=(rope, "rope" + tag), h=(hT, "hT" + tag), hT=(hT, "hT" + tag),
                        s1=(s1, "s1" + tag), r1=(r1, "r1" + tag), kf=(f, "kf" + tag), ka=(a, "ka" + tag),
                        kr=(r, "kr" + tag), s8=(s8, "s8" + tag), r8=(r8, "r8" + tag),
                        kst=(ks, "kst" + tag), vst=(vs_, "vst" + tag))

        def gen_prologue(kdst, kdname, vdst, vdname, t, B, store_eng="sp"):
            xsrc, xname = B["x"]
            rope_t, rname = B["rope"]
            hTb, hTn = B["hT"]
            kf_, kfn = B["kf"]
            kr_, krn = B["kr"]
            kst_, kstn = B["kst"]
            vst_, vstn = B["vst"]
            yield from gen_rmsnorm(xsrc, xname, hTb, hTn, B["h"][0], B["h"][1], B["s1"][0], B["s1"][1],
                                   B["r1"][0], B["r1"][1])
            yield
            pm, pname = proj(Wkv_full, "WinE", 1536, 256, hTb, hTn)
            P.op("dve", lambda e: e.tensor_copy(out=vst_[:], in_=pm[:, 128:256]), reads=[pname], writes=[vstn])
            P.op("dve", lambda e: e.tensor_copy(out=kf_[:, 0:128], in_=pm[:, 0:128]), reads=[pname], writes=[kfn])
            P.dma(store_eng, lambda e: e.dma_start(out=vdst[t * 128:(t + 1) * 128, :], in_=vst_[:]), vstn,
                  reads=[vstn], writes=[vdname + B["tag"]])
            yield
            yield from head_norm_rope((kf_, B["ka"][0]), (kfn, B["ka"][1]), 2, gk, "gk",
                                      rope_t, rname, kr_[:, 0:128].rearrange("p (h d) -> p h d", h=2), krn,
                                      B["s8"][0], B["r8"][0], B["s8"][1], B["r8"][1])
            yield
            yield
            P.op("pe", lambda e: e.transpose(out=psT[:, 0:128], in_=kr_[:, 0:128], identity=idb[:]),
                 reads=[krn, "idb"], writes=["psT"])
            P.op("dve", lambda e: e.tensor_copy(out=kst_[:], in_=psT[:, 0:128]), reads=["psT"], writes=[kstn])
            P.dma(store_eng, lambda e: e.dma_start(out=kdst[:, t * 128:(t + 1) * 128], in_=kst_[:]), kstn,
                  reads=[kstn], writes=[kdname + B["tag"]])
            yield

        BS_PRO = make_bufset("Q", xq, rpq, hT3, st1c, rs1c, kf, ka, kr, st8p, rs8p, kst, vst)
        BS_S1 = make_bufset("S", xs1, rp1, hT1, st1a, rs1a, qf, qa, qr, st8, rs8, kst2, vst2)
        P.alias.update({"xS": "xs1", "ropeS": "rp1", "hTS": "hT1", "s1S": "st1a", "r1S": "rs1a",
                        "kfS": "scr1", "kaS": "scr1", "krS": "qr", "s8S": "st8", "r8S": "rs8"})

        BS_P3 = make_bufset("P", xp, th[:, 0:128], hT2, st1, rs1, scr[:, 0:512], scr[:, 512:1024], mixT, den, rden,
                            diffT[:, 0:128], diffT[:, 128:256])
        P.alias.update({"xP": "xp", "ropeP": "th", "hTP": "hT2", "s1P": "st1", "r1P": "rs1", "kfP": "scr", "kaP": "scr",
                        "krP": "mixT", "s8P": "den", "r8P": "rden", "kstP": "diffT", "vstP": "diffT"})

        def gen_PRO(J, t):
            load_x(xq, "xQ", J["xs"], t, J["xsname"])
            load_rope(rpq, "ropeQ", t, J["rope"])
            yield
            yield from gen_prologue(J["kdst"], J["kdname"], J["vdst"], J["vdname"], t, BS_PRO)

        Wkv_full = WinE

        def gen_S1(J, t, src2d, sname, gi, nxt):
            NTq = J["ntq"]
            yield from gen_rmsnorm(xs1, "xs1", hT1, "hT1", hT1, "hT1", st1a, "st1a", rs1a, "rs1a")
            if nxt is not None:
                load_x(xs1, "xs1", nxt[2], nxt[1], nxt[3])
            yield
            pm, pname = proj(WinE, "WinE", 0, 512, hT1, "hT1")
            P.op("dve", lambda e, pm=pm: e.tensor_copy(out=pu[gi % 4][:], in_=pm[:]), reads=[pname], writes=[f"pu{gi % 4}"])
            yield
            pm, pname = proj(WinE, "WinE", 1024, 512, hT1, "hT1")
            P.op("dve", lambda e, pm=pm: e.tensor_copy(out=qf, in_=pm[:]), reads=[pname], writes=["qf"])
            yield
            qgen = head_norm_rope((qf, qa), ("qf", "qa"), 8, gq, "gq", rp1, "rp1",
                                  qr[:].rearrange("p (j g d) -> p g j d", j=4, g=2), "qr", st8, rs8, "st8", "rs8")
            next(qgen)
            yield
            pm, pname = proj(WinE, "WinE", 512, 512, hT1, "hT1")
            silu2(pm, pname, spz[gi % 3][:], f"spz{gi % 3}")
            yield
            next(qgen)
            yield
            pm, pname = proj(WinE, "WinE", 1792, 512, hT1, "hT1")
            silu2(pm, pname, saz[gi % 3][:], f"saz{gi % 3}")
            yield
            for _ in qgen:
                yield
            yield
            yield
            if nxt is not None:
                load_rope(rp1, "rp1", nxt[1], nxt[0]["rope"])
            s = gi % 2
            for j in range(4):
                P.op("pe", lambda e, j=j: e.transpose(out=psT[:, j * 128:(j + 1) * 128], in_=qr[:, j * 128:(j + 1) * 128],
                                                      identity=idb[:]),
                     reads=["qr", "idb"], writes=["psT"])
            P.op("dve", lambda e: e.tensor_copy(out=QTz[s][0][0:64, :], in_=psT[0:64, 0:512]), reads=["psT"],
                 writes=[f"QTz{s}"])
            P.op("dve", lambda e: e.tensor_copy(out=QTz[s][1][64:128, :], in_=psT[64:128, 0:512]), reads=["psT"],
                 writes=[f"QTz{s}"])
            yield

        def gen_ATT(J, t, gi):
            s = gi % 2
            NI = 2 * NT
            LAG = 2

            def qk(i):
                c, g = i // 2, i % 2
                b = i % 3
                P.op("pe", lambda e: e.matmul(psS[b][:], lhsT=KT[:, c * 128:(c + 1) * 128], rhs=QTz[s][g][:],
                                              start=True, stop=True),
                     reads=["KT", f"QTz{s}"], writes=[f"psS{b}"])
                P.op("act", lambda e: e.activation(out=pT[b][:], in_=psS[b][:], func=AF.Exp, scale=0.125),
                     reads=[f"psS{b}"], writes=[f"pT{b}"])

            def pv(i):
                c, g = i // 2, i % 2
                b = i % 3
                P.op("pe", lambda e: e.matmul(psO[g][0:VW, :], lhsT=V[:, c, g, :], rhs=pT[b][:],
                                              start=(c == 0), stop=(c == NT - 1)),
                     reads=["V", f"pT{b}"], writes=[f"psO{g}"])

            for i in range(NI + LAG):
                if i < NI:
                    qk(i)
                if i >= LAG:
                    pv(i - LAG)
                yield

        def gen_POST(J, t, src2d, sname, dst2d, dname_, gi):
            NTq = J["ntq"]
            bc_t, bc_n = J["bandc"], J["bcname"]
            load_x(xp, "xp", src2d, t, sname)
            for g in range(2):
                P.op("dve", lambda e, g=g: e.tensor_copy(out=oT[g][0:VW, :], in_=psO[g][0:VW, :]),
                     reads=[f"psO{g}"], writes=[f"oT{g}"])
            yield
            pm, pname = next_mm()
            var = 0 if t == 0 else (2 if t == NTq - 1 else 1)
            for g in range(4):
                hp = t > 0
                hn = t < NTq - 1
                P.op("pe", lambda e, g=g, pm=pm: e.matmul(pm[:, g * 128:(g + 1) * 128], lhsT=pu[gi % 4][:, g * 128:(g + 1) * 128],
                                                          rhs=bc_t[:, g * 3 + var, :], start=True, stop=not (hp or hn)),
                     reads=[f"pu{gi % 4}", bc_n], writes=[pname])
                if hp:
                    P.op("pe", lambda e, g=g, pm=pm: e.matmul(pm[:, g * 128:g * 128 + 8],
                                                              lhsT=pu[(gi - 1) % 4][:, g * 128:(g + 1) * 128],
                                                              rhs=bandp[:, g, :], start=False, stop=not hn),
                         reads=[f"pu{(gi - 1) % 4}", "bandp"], writes=[pname])
                if hn:
                    P.op("pe", lambda e, g=g, pm=pm: e.matmul(pm[:, g * 128 + 120:g * 128 + 128],
                                                              lhsT=pu[(gi + 1) % 4][:, g * 128:(g + 1) * 128],
                                                              rhs=bandn[:, g, :], start=False, stop=True),
                         reads=[f"pu{(gi + 1) % 4}", "bandn"], writes=[pname])
            P.op("dve", lambda e, pm=pm: e.tensor_copy(out=diffT[:], in_=pm[:]), reads=[pname], writes=["diffT"])
            yield
            pm, pname = next_mm()
            for g in range(4):
                P.op("pe", lambda e, g=g, pm=pm: e.matmul(pm[:, g * 128:(g + 1) * 128], lhsT=diffT[:, g * 128:(g + 1) * 128],
                                                          rhs=poolW[:, g, :], start=True, stop=True),
                     reads=["diffT", "poolW"], writes=[pname])
            P.op("dve", lambda e, pm=pm: e.tensor_tensor(out=mix[:, 0:512], in0=pm[:], in1=spz[gi % 3][:], op=ALU.mult),
                 reads=[pname, f"spz{gi % 3}"], writes=["mix"])
            yield
            for g in range(2):
                pm, pname = next_mm()
                for j in range(4):
                    P.op("pe", lambda e, g=g, j=j, pm=pm: e.transpose(out=pm[:, j * VW:(j + 1) * VW],
                                                                      in_=oT[g][0:VW, j * 128:(j + 1) * 128],
                                                                      identity=idf[0:VW, 0:VW]),
                         reads=[f"oT{g}", "idf"], writes=[pname])
                dview = pm[:, 0:4 * VW].rearrange("p (j w) -> p j w", w=VW)[:, :, 64]
                P.op("dve", lambda e, g=g, dview=dview: e.tensor_scalar(out=den[:, g * 4:(g + 1) * 4], in0=dview, scalar1=2.0,
                                                                        scalar2=None, op0=ALU.mult),
                     reads=[pname], writes=["den"])
                P.op("dve", lambda e, g=g: e.reciprocal(out=rden[:, g * 4:(g + 1) * 4], in_=den[:, g * 4:(g + 1) * 4]),
                     reads=["den"], writes=["rden"])
                for j in range(4):
                    hh = 4 * g + j
                    P.op("dve", lambda e, pm=pm, j=j, hh=hh: e.scalar_tensor_tensor(
                        out=mix[:, 512 + hh * 64:512 + (hh + 1) * 64], in0=pm[:, j * VW:j * VW + 64],
                        scalar=rden[:, hh:hh + 1], in1=saz[gi % 3][:, hh * 64:(hh + 1) * 64], op0=ALU.mult, op1=ALU.mult),
                        reads=[pname, "rden", f"saz{gi % 3}"], writes=["mix"])
                yield
            yield
            yield
            yield
            for k in range(8):
                P.op("pe", lambda e, k=k: e.transpose(out=psT[:, k * 128:(k + 1) * 128], in_=mix[:, k * 128:(k + 1) * 128],
                                                      identity=idb[:]),
                     reads=["mix", "idb"], writes=["psT"])
            P.op("dve", lambda e: e.tensor_copy(out=mixT[:], in_=psT[:]), reads=["psT"], writes=["mixT"])
            yield
            yield
            for sl in range(2):
                pm, pname = proj(WoutE, "WoutE", sl * 512, 512, mixT, "mixT")
                P.op("dve", lambda e, pm=pm, sl=sl: e.tensor_tensor(out=xp[:, sl * 512:(sl + 1) * 512], in0=pm[:],
                                                                    in1=xp[:, sl * 512:(sl + 1) * 512], op=ALU.add),
                     reads=[pname, "xp"], writes=["xp"])
                yield
            yield
            yield from gen_rmsnorm(xp, "xp", hT2, "hT2", hT2, "hT2", st1, "st1", rs1, "rs1", ny=3)
            yield
            for sl in range(2):
                pm, pname = proj(WinO, "WinO", sl * 512, 512, hT2, "hT2")
                P.op("dve", lambda e, pm=pm, sl=sl: e.tensor_copy(out=u[:, sl * 512:(sl + 1) * 512], in_=pm[:]),
                     reads=[pname], writes=["u"])
                yield
            for sl in range(2):
                pm, pname = proj(WinO, "WinO", 1024 + sl * 512, 512, hT2, "hT2")
                P.op("dve", lambda e, pm=pm, sl=sl: e.tensor_copy(out=vv[:, sl * 512:(sl + 1) * 512], in_=pm[:]),
                     reads=[pname], writes=["vv"])
                yield
            P.op("act", lambda e: e.activation(out=vv, in_=vv, func=AF.Gelu), reads=["vv"], writes=["vv"])
            P.op("act", lambda e: e.activation(out=u, in_=u, func=AF.Gelu), reads=["u"], writes=["u"])
            for sl in range(2):
                pm, pname = proj(WinO, "WinO", 2048 + sl * 512, 512, hT2, "hT2")
                silu2(pm, pname, tmpf[:], "tmpf")
                P.op("pool", lambda e, sl=sl: e.tensor_tensor(out=u[:, sl * 512:(sl + 1) * 512], in0=u[:, sl * 512:(sl + 1) * 512],
                                                              in1=tmpf[:], op=ALU.mult),
                     reads=["u", "tmpf"], writes=["u"])
                yield
            P.op("dve", lambda e: e.scalar_tensor_tensor(out=vvn[:], in0=vv, scalar=1.0, in1=vv, op0=ALU.mult,
                                                         op1=ALU.mult, accum_out=st1[:]),
                 reads=["vv"], writes=["vvn", "st1"])
            yield
            P.op("dve", lambda e: e.tensor_scalar(out=rs1[:], in0=st1[:], scalar1=1.0 / D, scalar2=EPS, op0=ALU.mult,
                                                  op1=ALU.add),
                 reads=["st1"], writes=["rs1"])
            P.op("pool", lambda e: e.tensor_tensor(out=rs1[:], in0=rs1[:], in1=mhalf[:, 0:1], op=ALU.pow),
                 reads=["rs1", "mhalf"], writes=["rs1"])
            yield
            P.op("dve", lambda e: e.scalar_tensor_tensor(out=vvn[:], in0=vv, scalar=rs1[:], in1=gS[:], op0=ALU.mult,
                                                         op1=ALU.mult),
                 reads=["vv", "rs1", "gS"], writes=["vvn"])
            yield
            yield
            for half in range(2):
                pm, pname = next_mm()
                for j in range(4):
                    hh = half * 4 + j
                    P.op("pe", lambda e, pm=pm, j=j, hh=hh: e.matmul(pm[:, j * 128:(j + 1) * 128], lhsT=WsT[:, hh, :],
                                                                    rhs=vvn[:, hh * 128:(hh + 1) * 128], start=True, stop=True),
                         reads=["WsT", "vvn"], writes=[pname])
                for j in range(4):
                    hh = half * 4 + j
                    P.op("dve", lambda e, pm=pm, j=j, hh=hh: e.scalar_tensor_tensor(
                        out=mix[:, hh * 128:(hh + 1) * 128], in0=pm[:, j * 128:(j + 1) * 128], scalar=bsT[:, hh:hh + 1],
                        in1=u[:, hh * 128:(hh + 1) * 128], op0=ALU.add, op1=ALU.mult),
                        reads=[pname, "bsT", "u"], writes=["mix"])
                yield
            yield
            yield
            for k in range(8):
                P.op("pe", lambda e, k=k: e.transpose(out=psT[:, k * 128:(k + 1) * 128], in_=mix[:, k * 128:(k + 1) * 128],
                                                      identity=idb[:]),
                     reads=["mix", "idb", "vvn"], writes=["psT"])
            P.op("dve", lambda e: e.tensor_copy(out=mixT[:], in_=psT[:]), reads=["psT"], writes=["mixT"])
            yield
            yield
            for sl in range(2):
                pm, pname = proj(WoutO, "WoutO", sl * 512, 512, mixT, "mixT")
                P.op("dve", lambda e, pm=pm, sl=sl: e.scalar_tensor_tensor(
                    out=xp[:, sl * 512:(sl + 1) * 512], in0=pm[:], scalar=0.5, in1=xp[:, sl * 512:(sl + 1) * 512],
                    op0=ALU.mult, op1=ALU.add),
                    reads=[pname, "xp"], writes=["xp"])
                yield
            P.dma("sp", lambda e: e.dma_start(out=dst2d[t * 128:(t + 1) * 128, :], in_=xp[:]), "stx",
                  reads=["xp"], writes=[dname_])

        def load_kv(J, layer_pair):
            P.op("pool", lambda e: e.memset(V[:], 1.0), writes=["V"])
            if J["half"] and layer_pair > 0:
                kg, vg = kvg_k.ap(), kvg_v.ap()
                o2 = (NTL - HT) * 128
                P.dma("sp", lambda e: [e.dma_start(out=KT[:, 0:HT * 128], in_=kg[0:128, 0:HT * 128]),
                                       e.dma_start(out=KT[:, HT * 128:L], in_=kg[128:256, o2:NTL * 128])],
                      "KT", reads=["kvg_k"], writes=["KT"], n=2)

                def vfn(e):
                    res = []
                    for g in range(2):
                        res.append(e.dma_start(out=V[:, 0:HT, g, 0:64],
                                               in_=vg[0:HT * 128, g * 64:(g + 1) * 64].rearrange("(c p) d -> p c d", p=128)))
                        res.append(e.dma_start(out=V[:, HT:NT, g, 0:64],
                                               in_=vg[NTL * 128 + o2:2 * NTL * 128, g * 64:(g + 1) * 64]
                                               .rearrange("(c p) d -> p c d", p=128)))
                    return res
                P.dma("sp", vfn, "V", reads=["kvg_v"], writes=["V"], n=4)
            else:
                ksrc = kts_d[J["pa"]]
                vsrc = vs_d[J["pa"]]
                P.dma("sp", lambda e: e.dma_start(out=KT[:], in_=ksrc), "KT", reads=["kts_d" + x_ for x_ in "QSP"],
                      writes=["KT"])
                P.dma("sp", lambda e: [e.dma_start(out=V[:, :, g, 0:64],
                                                   in_=vsrc[:, g * 64:(g + 1) * 64].rearrange("(c p) d -> p c d", p=128))
                                       for g in range(2)],
                      "V", reads=["vs_d" + x_ for x_ in "QSP"], writes=["V"], n=2)

        def count_steps(genfn):
            saved = (P.op, P.dma, mmi[0])
            P.op = lambda *a, **k: None
            P.dma = lambda *a, **k: None
            n = sum(1 for _ in genfn())
            P.op, P.dma = saved[0], saved[1]
            mmi[0] = saved[2]
            return n

        FRAC = 94

        def run_slot(gens):
            gens = [(g, n) for g, n in gens if g is not None]
            if not gens:
                return
            main, nmain = gens[0]
            others = gens[1:]
            done = [0] * len(others)
            for oi, (g, n) in enumerate(others):
                if next(g, "END") != "END":
                    done[oi] += 1
            for i in range(nmain):
                next(main, None)
                for oi, (g, n) in enumerate(others):
                    target = ((i + 1) * n * 100 + nmain * FRAC - 1) // (nmain * FRAC)
                    while done[oi] < min(target, n):
                        next(g, None)
                        done[oi] += 1
            for oi, (g, n) in enumerate(others):
                for _ in g:
                    pass
            for _ in main:
                pass

        npairs = depth // 2
        issue_pair_dmas(0)
        post_winE()
        bcastload(gk, "gk", k_norm[0:1, :])
        def gen_passA(seq, tiles, B):
            for t in tiles:
                load_x(B["x"][0], B["x"][1], x_d[seq], t)
                load_rope(B["rope"][0], B["rope"][1], t)
                yield from gen_prologue(kts_d[seq], "kts_d", vs_d[seq], "vs_d", t, B, store_eng="act")

        for seq in range(NSEQ):
            ga = gen_passA(seq, range(0, NT, 3), BS_PRO)
            gb = gen_passA(seq, range(1, NT, 3), BS_S1)
            gc_ = gen_passA(seq, range(2, NT, 3), BS_P3)
            for _ in range(8):
                next(ga, None)
            for _ in range(4):
                next(gb, None)
            alive = True
            while alive:
                alive = False
                for g in (ga, gb, gc_):
                    if next(g, "END") != "END":
                        alive = True
        post_rest()
        for j in range(npairs):
            last = (j == npairs - 1)
            if j > 0:
                issue_pair_dmas(j)
                post_winE()
                post_rest()
            if not last:
                load_kv_weights(j + 1)
            items = []
            for J in JOBS:
                src2d, sname = (J["xin"], None) if j == 0 else (J["xs"], J["xsname"])
                dst2d, dname_ = (J["out"], "out_" + J["name"]) if last else (J["xs"], J["xsname"])
                for t in range(J["ntq"]):
                    items.append((J, t, src2d, sname, dst2d, dname_))
            NI_ = len(items)
            I0 = items[0]
            load_x(xs1, "xs1", I0[2], I0[1], I0[3])
            load_rope(rp1, "rp1", I0[1], I0[0]["rope"])
            nS1 = count_steps(lambda: gen_S1(I0[0], 1, I0[2], I0[3], 1, None))
            nATT = count_steps(lambda: gen_ATT(I0[0], 1, 1))
            nPOST = count_steps(lambda: gen_POST(I0[0], 1, I0[2], I0[3], I0[4], I0[5], 1))
            nPRO = count_steps(lambda: gen_PRO(I0[0], 1))
            for k in range(-1, NI_ + 2):
                att = post = s1 = pro = (None, 0)
                if 0 <= k < NI_:
                    J, t = items[k][0], items[k][1]
                    if t == 0:
                        load_kv(J, j)
                    att = (gen_ATT(J, t, k), nATT)
                if 0 <= k - 1 < NI_:
                    J, t, a, b, c_, d_ = items[k - 1]
                    post = (gen_POST(J, t, a, b, c_, d_, k - 1), nPOST)
                if 0 <= k + 1 < NI_:
                    J, t, a, b, c_, d_ = items[k + 1]
                    nx = items[k + 2] if k + 2 < NI_ else None
                    s1 = (gen_S1(J, t, a, b, k + 1, None if nx is None else (nx[0], nx[1], nx[2], nx[3])), nS1)
                if (not last) and 0 <= k - 2 < NI_:
                    pro = (gen_PRO(items[k - 2][0], items[k - 2][1]), nPRO)
                if att[0] is not None:
                    run_slot([att, post, s1, pro])
                else:
                    run_slot([post, s1, pro])
            if half and not last:
                groups = [[c, c + 4] for c in range(4)]
                P.cc(lambda e: e.collective_compute("AllGather", ALU.bypass, replica_groups=groups,
                                                    ins=[kvl_k.ap().opt()], outs=[kvg_k.ap().opt()]),
                     "cck", reads=["kvl_k" + x_ for x_ in "QSP"], writes=["kvg_k"])
                P.cc(lambda e: e.collective_compute("AllGather", ALU.bypass, replica_groups=groups,
                                                    ins=[kvl_v.ap().opt()], outs=[kvg_v.ap().opt()]),
                     "ccv", reads=["kvl_v" + x_ for x_ in "QSP"], writes=["kvg_v"])
        P.emit()
    return nc


_CACHE = {}


_BANDS = band_tables()


def _consts(L):
    return {
        "ident_b": np.eye(128, dtype=np.float32).astype(ml_dtypes.bfloat16),
        "ident_f": np.eye(128, dtype=np.float32),
        "rope": rope_table(L),
        "band_c": _BANDS[0], "band_p": _BANDS[1], "band_n": _BANDS[2],
    }


def run_step(x_prompt, x_sample, weights, depth=4):
    L = x_prompt.shape[1]
    NT = L // 128
    NTL = half_tiles(NT)
    HT = NT // 2
    key = (L, depth)
    if key not in _CACHE:
        _CACHE[key] = build_program(L, 2, depth, half=True)
    nc = _CACHE[key]
    consts = _consts(L)
    bc = consts["band_c"]
    in_maps = []
    for c in range(N_CORES):
        lower = c < 4
        g0 = 0 if lower else NT - NTL
        xs_ = x_sample[c % 4]
        m = {"x": np.ascontiguousarray(np.stack([x_prompt[c], xs_], axis=0), dtype=np.float32),
             "xh": np.ascontiguousarray(xs_[g0 * 128:(g0 + NTL) * 128], dtype=np.float32),
             "rope_h": np.ascontiguousarray(consts["rope"][g0 * 128:(g0 + NTL) * 128])}
        sel = [0, 1, 1] if lower else [1, 1, 2]
        m["band_ch"] = np.ascontiguousarray(bc[:, [g * 3 + v for g in range(4) for v in sel], :])
        for k, v in weights.items():
            m[k] = np.ascontiguousarray(v, dtype=np.float32)
        m.update(consts)
        in_maps.append(m)
    res = run_bass_kernel_spmd(nc, in_maps, core_ids=list(range(N_CORES)))
    outs = res.results
    y_prompt = np.stack([np.asarray(outs[c]["out"]) for c in range(N_CORES)], axis=0).astype(np.float32)
    y_sample = np.stack([np.concatenate([np.asarray(outs[s_]["out_h"])[0:HT * 128],
                                         np.asarray(outs[s_ + 4]["out_h"])[(NTL - HT) * 128:NTL * 128]], axis=0)
                         for s_ in range(4)], axis=0).astype(np.float32)
    return y_prompt, y_sample


def kernel(x_prompt, x_sample, norm_e, w_in_e, pool_w, pool_scale, q_norm, k_norm, w_out_e,
           norm_o, w_in_o, sgu_norm, w_s, b_s, w_out_o):
    x_prompt = np.asarray(x_prompt)
    x_sample = np.asarray(x_sample)
    weights = dict(norm_e=norm_e, w_in_e=w_in_e, pool_w=pool_w, pool_scale=pool_scale, q_norm=q_norm,
                   k_norm=k_norm, w_out_e=w_out_e, norm_o=norm_o, w_in_o=w_in_o, sgu_norm=sgu_norm,
                   w_s=w_s, b_s=b_s, w_out_o=w_out_o)
    weights = {k: np.asarray(v) for k, v in weights.items()}
    return run_step(x_prompt, x_sample, weights)
```
